# Optimizing a Trainium2 kernel written in Bass

```python
import math
import jax, jax.numpy as jnp
from jax import lax
import numpy as np


D_MODEL = 2048
BATCH = 4
SEQ = 4096
DEPTH = 1

MEM_LEN = 256
EPS = 1e-6
DA_HEADS = 4
DA_HEAD_DIM = D_MODEL // (4 * DA_HEADS)
DA_WIDTH = DA_HEADS * 2 * DA_HEAD_DIM
Q_BLOCK = 128
HG_HEADS = 8
HG_KEY_DIM = 128
HG_VAL_DIM = D_MODEL // (2 * HG_HEADS)
HG_QK = HG_HEADS * HG_KEY_DIM
HG_WIDTH = HG_HEADS * HG_VAL_DIM
HG_CHUNK = 64
XA_HEADS = 4
XA_HEAD_DIM = D_MODEL // (2 * XA_HEADS)
XA_WIDTH = XA_HEADS * XA_HEAD_DIM
N_BRANCH = 3
IN_SIZES = (DA_WIDTH, DA_WIDTH, DA_WIDTH, HG_QK, HG_QK, HG_WIDTH, HG_WIDTH, XA_WIDTH, N_BRANCH * D_MODEL)
IN_COLS = 3 * DA_WIDTH + 2 * HG_QK + 2 * HG_WIDTH + XA_WIDTH + N_BRANCH * D_MODEL
D_FF = 5632

kernel_name = "hybrid_diffattn_hgrn2_memxattn_macaron"


def rmsnorm(x, g):
    xf = x.astype(jnp.float32)
    y = xf * lax.rsqrt(jnp.mean(xf * xf, axis=-1, keepdims=True) + EPS)
    return (y * g.astype(jnp.float32)).astype(x.dtype)


def swiglu(x, w_gate, w_up, w_down):
    return (jax.nn.silu(x @ w_gate) * (x @ w_up)) @ w_down


def alibi_slopes(n):
    return jnp.array([2.0 ** (-8.0 * (i + 1) / n) for i in range(n)], dtype=jnp.float32)


def split_cols(t, sizes):
    out, start = [], 0
    for n in sizes:
        out.append(t[..., start:start + n])
        start += n
    return out


def diff_attention(q, k, v, lam_q1, lam_k1, lam_q2, lam_k2, subln_g, lambda_init):
    B, S, _ = q.shape
    f32 = jnp.float32
    q = q.reshape(B, S, DA_HEADS, 2, DA_HEAD_DIM) * (DA_HEAD_DIM ** -0.5)
    k = k.reshape(B, S, DA_HEADS, 2, DA_HEAD_DIM)
    v = v.reshape(B, S, DA_HEADS, 2 * DA_HEAD_DIM)
    lam = (jnp.exp(jnp.sum(lam_q1.astype(f32) * lam_k1.astype(f32)))
           - jnp.exp(jnp.sum(lam_q2.astype(f32) * lam_k2.astype(f32))) + lambda_init)
    slopes = alibi_slopes(DA_HEADS)
    n_blocks = S // Q_BLOCK
    q_blocks = q.reshape(B, n_blocks, Q_BLOCK, DA_HEADS, 2, DA_HEAD_DIM).transpose(1, 0, 2, 3, 4, 5)
    k_pos = jnp.arange(S)

    def one_block(args):
        q_blk, blk = args
        s = jnp.einsum('bqhcd,bkhcd->bhcqk', q_blk, k, preferred_element_type=f32)
        q_pos = blk * Q_BLOCK + jnp.arange(Q_BLOCK)
        dist = q_pos[:, None] - k_pos[None, :]
        bias = -slopes[:, None, None] * dist.astype(f32)[None]
        s = jnp.where((dist >= 0)[None, None, None], s + bias[None, :, None], -jnp.inf)
        p = jax.nn.softmax(s, axis=-1)
        a = p[:, :, 0] - lam * p[:, :, 1]
        return jnp.einsum('bhqk,bkhe->bqhe', a.astype(v.dtype), v)

    o = lax.map(one_block, (q_blocks, jnp.arange(n_blocks)))
    o = o.transpose(1, 0, 2, 3, 4).reshape(B, S, DA_HEADS, 2 * DA_HEAD_DIM)
    o = rmsnorm(o, subln_g) * (1.0 - lambda_init)
    return o.reshape(B, S, DA_WIDTH)


def hgrn2(q, f_logit, i_in, o_gate, lb, norm_g):
    B, S, _ = q.shape
    dt = i_in.dtype
    f32 = jnp.float32
    z = f_logit.astype(f32).reshape(B, S, HG_HEADS, HG_KEY_DIM)
    lb = lb.astype(f32).reshape(HG_HEADS, HG_KEY_DIM)
    log_f = jnp.log(lb + (1.0 - lb) * jax.nn.sigmoid(z))
    kk = (1.0 - lb) * jax.nn.sigmoid(-z)
    qf = jax.nn.silu(q.astype(f32)).reshape(B, S, HG_HEADS, HG_KEY_DIM)
    v = i_in.astype(f32).reshape(B, S, HG_HEADS, HG_VAL_DIM)
    n_chunks = S // HG_CHUNK

    def chunked(t):
        return t.reshape(B, n_chunks, HG_CHUNK, HG_HEADS, t.shape[-1]).transpose(1, 0, 3, 2, 4)

    causal = jnp.tril(jnp.ones((HG_CHUNK, HG_CHUNK), dtype=bool))[:, :, None]

    def step(state, xs):
        qc, kc, gc, vc = xs
        b = lax.cumsum(gc, axis=2)
        pair = b[:, :, :, None, :] - b[:, :, None, :, :]
        decay = jnp.where(causal, jnp.exp(jnp.where(causal, pair, 0.0)), 0.0)
        scores = jnp.einsum('bhtsk,bhsk->bhts', qc[:, :, :, None, :] * decay, kc)
        o = (jnp.einsum('bhts,bhsv->bhtv', scores, vc)
             + jnp.einsum('bhtk,bhkv->bhtv', qc * jnp.exp(b), state))
        b_end = b[:, :, -1:, :]
        state = (jnp.exp(b_end[:, :, 0, :])[..., None] * state
                 + jnp.einsum('bhsk,bhsv->bhkv', kc * jnp.exp(b_end - b), vc))
        return state, o

    state0 = jnp.zeros((B, HG_HEADS, HG_KEY_DIM, HG_VAL_DIM), f32)
    _, o = lax.scan(step, state0, (chunked(qf), chunked(kk), chunked(log_f), chunked(v)))
    o = o.transpose(1, 0, 3, 2, 4).reshape(B, S, HG_HEADS, HG_VAL_DIM)
    o = o * lax.rsqrt(jnp.mean(o * o, axis=-1, keepdims=True) + EPS) * norm_g.astype(f32).reshape(HG_HEADS, HG_VAL_DIM)
    o = o * jax.nn.sigmoid(o_gate.astype(f32)).reshape(B, S, HG_HEADS, HG_VAL_DIM)
    return o.reshape(B, S, HG_WIDTH).astype(dt)


def memory_cross_attention(q, mem_n, w_mem_kv):
    B, S, _ = q.shape
    f32 = jnp.float32
    kv = mem_n @ w_mem_kv
    k, v = kv[..., :XA_WIDTH], kv[..., XA_WIDTH:]
    q = q.reshape(B, S, XA_HEADS, XA_HEAD_DIM) * (XA_HEAD_DIM ** -0.5)
    k = k.reshape(B, -1, XA_HEADS, XA_HEAD_DIM)
    v = v.reshape(B, -1, XA_HEADS, XA_HEAD_DIM)
    s = jnp.einsum('bqhd,bmhd->bhqm', q, k, preferred_element_type=f32)
    p = jax.nn.softmax(s, axis=-1)
    o = jnp.einsum('bhqm,bmhd->bqhd', p.astype(v.dtype), v)
    return o.reshape(B, S, XA_WIDTH)


def setup_inputs(seed: int = 0) -> dict:
    key = jax.random.key(seed)
    ks = jax.random.split(key, 32)
    f32 = jnp.float32
    L = DEPTH

    def nrm(k, shape, scale):
        return jax.random.normal(k, shape, f32) * scale

    def gain(k, shape):
        return 1.0 + 0.02 * jax.random.normal(k, shape, f32)

    return {
        'x': nrm(ks[0], (BATCH, SEQ, D_MODEL), 1.0),
        'mem': nrm(ks[1], (BATCH, MEM_LEN, D_MODEL), 1.0),
        'ffn1_norm': gain(ks[2], (L, D_MODEL)),
        'ffn1_w_gate': nrm(ks[3], (L, D_MODEL, D_FF), D_MODEL ** -0.5),
        'ffn1_w_up': nrm(ks[4], (L, D_MODEL, D_FF), D_MODEL ** -0.5),
        'ffn1_w_down': nrm(ks[5], (L, D_FF, D_MODEL), D_FF ** -0.5),
        'mix_norm': gain(ks[6], (L, D_MODEL)),
        'mem_norm': gain(ks[7], (L, D_MODEL)),
        'w_in': nrm(ks[8], (L, D_MODEL, IN_COLS), D_MODEL ** -0.5),
        'da_lambda_q1': nrm(ks[9], (L, DA_HEAD_DIM), 0.1),
        'da_lambda_k1': nrm(ks[10], (L, DA_HEAD_DIM), 0.1),
        'da_lambda_q2': nrm(ks[11], (L, DA_HEAD_DIM), 0.1),
        'da_lambda_k2': nrm(ks[12], (L, DA_HEAD_DIM), 0.1),
        'da_subln': gain(ks[13], (L, 2 * DA_HEAD_DIM)),
        'hg_lb_logits': nrm(ks[14], (L + 1, HG_QK), 0.1),
        'hg_norm': gain(ks[15], (L, HG_WIDTH)),
        'w_mem_kv': nrm(ks[16], (L, D_MODEL, 2 * XA_WIDTH), D_MODEL ** -0.5),
        'w_branch_da': nrm(ks[17], (L, DA_WIDTH, D_MODEL), DA_WIDTH ** -0.5),
        'w_branch_hg': nrm(ks[18], (L, HG_WIDTH, D_MODEL), HG_WIDTH ** -0.5),
        'w_branch_xa': nrm(ks[19], (L, XA_WIDTH, D_MODEL), XA_WIDTH ** -0.5),
        'w_out': nrm(ks[20], (L, D_MODEL, D_MODEL), D_MODEL ** -0.5),
        'ffn2_norm': gain(ks[21], (L, D_MODEL)),
        'ffn2_w_gate': nrm(ks[22], (L, D_MODEL, D_FF), D_MODEL ** -0.5),
        'ffn2_w_up': nrm(ks[23], (L, D_MODEL, D_FF), D_MODEL ** -0.5),
        'ffn2_w_down': nrm(ks[24], (L, D_FF, D_MODEL), D_FF ** -0.5),
        'final_norm': gain(ks[25], (D_MODEL,)),
    }


def reference(x, mem, ffn1_norm, ffn1_w_gate, ffn1_w_up, ffn1_w_down, mix_norm, mem_norm, w_in,
              da_lambda_q1, da_lambda_k1, da_lambda_q2, da_lambda_k2, da_subln, hg_lb_logits, hg_norm,
              w_mem_kv, w_branch_da, w_branch_hg, w_branch_xa, w_out,
              ffn2_norm, ffn2_w_gate, ffn2_w_up, ffn2_w_down, final_norm):
    B, S, D = x.shape
    lower_bounds = jnp.cumsum(jax.nn.softmax(hg_lb_logits.astype(jnp.float32), axis=0), axis=0)
    h = x
    for l in range(DEPTH):
        lambda_init = 0.8 - 0.6 * math.exp(-0.3 * l)
        h = h + 0.5 * swiglu(rmsnorm(h, ffn1_norm[l]), ffn1_w_gate[l], ffn1_w_up[l], ffn1_w_down[l])
        u = rmsnorm(h, mix_norm[l])
        proj = u @ w_in[l]
        da_q, da_k, da_v, hg_q, hg_f, hg_i, hg_g, xa_q, gate_logits = split_cols(proj, IN_SIZES)
        y_da = diff_attention(da_q, da_k, da_v, da_lambda_q1[l], da_lambda_k1[l], da_lambda_q2[l],
                              da_lambda_k2[l], da_subln[l], lambda_init)
        y_hg = hgrn2(hg_q, hg_f, hg_i, hg_g, lower_bounds[l], hg_norm[l])
        y_xa = memory_cross_attention(xa_q, rmsnorm(mem, mem_norm[l]), w_mem_kv[l])
        gates = jax.nn.sigmoid(gate_logits.astype(jnp.float32)).astype(h.dtype).reshape(B, S, N_BRANCH, D)
        merged = (gates[:, :, 0] * (y_da @ w_branch_da[l])
                  + gates[:, :, 1] * (y_hg @ w_branch_hg[l])
                  + gates[:, :, 2] * (y_xa @ w_branch_xa[l]))
        h = h + merged @ w_out[l]
        h = h + 0.5 * swiglu(rmsnorm(h, ffn2_norm[l]), ffn2_w_gate[l], ffn2_w_up[l], ffn2_w_down[l])
    return rmsnorm(h, final_norm)
```

```python
import numpy as np
import concourse.bass as bass
import concourse.mybir as mybir
from concourse.bass_utils import run_bass_kernel_spmd
from contextlib import ExitStack

F32 = mybir.dt.float32
BF16 = mybir.dt.bfloat16
AF = mybir.ActivationFunctionType
ALU = mybir.AluOpType
AX = mybir.AxisListType

D = 2048
DFF = 5632
NCH = 16
NHC = 44
T = 512
OWN = 2048
NT = OWN // T
EPS = 1e-6
LAMBDA_INIT = 0.2
SLOPES = [2.0 ** (-8.0 * (i + 1) / 4) for i in range(4)]
NEG = -30000.0
NSLOT = 3
SLOT_ELEMS = 8192

WNAMES = ["ffn1_wg", "ffn1_wu", "ffn1_wd", "w_in", "w_mem_kv", "wb0", "wb1", "wb2", "w_out",
          "ffn2_wg", "ffn2_wu", "ffn2_wd"]
WSHAPES = {"ffn1_wg": (D, DFF), "ffn1_wu": (D, DFF), "ffn1_wd": (DFF, D), "w_in": (D, 14336),
           "w_mem_kv": (D, 2048), "wb0": (1024, D), "wb1": (1024, D), "wb2": (1024, D), "w_out": (D, D),
           "ffn2_wg": (D, DFF), "ffn2_wu": (D, DFF), "ffn2_wd": (DFF, D)}


class Sem:
    __slots__ = ("h", "id")

    def __init__(self, h, i):
        self.h = h
        self.id = i


class Buf:
    __slots__ = ("name", "w", "r", "sem_in", "sem_out", "cnt_in", "cnt_out")

    def __init__(self, name):
        self.name = name
        self.w = None
        self.r = {}
        self.sem_in = None
        self.sem_out = None
        self.cnt_in = 0
        self.cnt_out = 0


class Eng:
    def __init__(self, K, name, h):
        self.K = K
        self.name = name
        self.h = h
        self.sem = None
        self.cnt = 0
        self.waited = {}
        self.last = None

    def wait(self, ev):
        sem, val = ev
        if self.waited.get(sem.id, 0) >= val:
            return
        self.waited[sem.id] = val
        self.h.wait_ge(sem.h, val)

    def tick(self, inst):
        if self.sem is None or self.cnt >= 30000:
            self.sem = self.K.alloc_sem(self.name)
            self.cnt = 0
        self.cnt += 1
        inst.then_inc(self.sem.h, 1)
        self.last = (self.sem, self.cnt)
        return self.last


class Tracker:
    def __init__(self, nc, es, dry):
        self.nc = nc
        self.es = es
        self.dry = dry
        self.nsem = 0
        self.pe = Eng(self, "pe", nc.tensor)
        self.act = Eng(self, "act", nc.scalar)
        self.dve = Eng(self, "dve", nc.vector)
        self.pool = Eng(self, "pool", nc.gpsimd)
        self.sp = Eng(self, "sp", nc.sync)
        self.sp_dma = {}
        self.grp = {}
        self.n_inst = 0

    def alloc_sem(self, name):
        self.nsem += 1
        h = self.es.enter_context(self.nc.semaphore(f"s_{name}_{self.nsem}"))
        return Sem(h, self.nsem)

    def _deps(self, eng, reads, writes):
        for b in reads:
            if b.w is not None:
                eng.wait(b.w)
        for b in writes:
            if b.w is not None:
                eng.wait(b.w)
            for ev in b.r.values():
                eng.wait(ev)

    def op(self, eng, fn, reads=(), writes=()):
        if self.dry:
            return
        self._deps(eng, reads, writes)
        inst = fn()
        ev = eng.tick(inst)
        self.n_inst += 1
        for b in reads:
            b.r[ev[0].id] = ev
        for b in writes:
            b.w = ev
            b.r = {}

    def mm(self, out, lhsT, rhs, start, stop, reads, pbuf, tick=False):
        if self.dry:
            return
        pe = self.pe
        for b in reads:
            if b.w is not None and b.w[0] is not pe.sem:
                pe.wait(b.w)
        if start:
            if pbuf.w is not None and pbuf.w[0] is not pe.sem:
                pe.wait(pbuf.w)
            for ev in pbuf.r.values():
                pe.wait(ev)
            self.grp[id(pbuf)] = set()
        g = self.grp[id(pbuf)]
        for b in reads:
            g.add(b)
        inst = self.nc.tensor.matmul(out, lhsT=lhsT, rhs=rhs, start=start, stop=stop)
        self.n_inst += 1
        if stop or tick:
            ev = pe.tick(inst)
            for b in g:
                b.r[ev[0].id] = ev
            if stop:
                pbuf.w = ev
                pbuf.r = {}
            else:
                g.clear()
                g.update(())

    def dma(self, eng, out, in_, reads=(), writes=(), track=True, **kw):
        if self.dry:
            return
        self._deps(eng, reads, writes)
        if writes:
            b = writes[0]
            if b.sem_in is None:
                b.sem_in = self.alloc_sem("di")
            b.cnt_in += 16
            ev = (b.sem_in, b.cnt_in)
        else:
            b = reads[0]
            if b.sem_out is None:
                b.sem_out = self.alloc_sem("do")
            b.cnt_out += 16
            ev = (b.sem_out, b.cnt_out)
        eng.h.dma_start(out=out, in_=in_, **kw).then_inc(ev[0].h, 16)
        self.n_inst += 1
        for x in reads:
            x.r[ev[0].id] = ev
        for x in writes:
            x.w = ev
            x.r = {}
        if track:
            self.sp_dma[ev[0].id] = ev

    def barrier(self):
        if self.dry:
            return
        evs = [e.last for e in (self.pe, self.act, self.dve, self.pool) if e.last is not None]
        evs += list(self.sp_dma.values())
        for e in (self.pe, self.act, self.dve, self.sp):
            for ev in evs:
                e.wait(ev)
        self.sp_dma = {}


def build(dry, plan, debug_stage=None):
    nc = bass.Bass("TRN2", target_bir_lowering=False)
    es = ExitStack()
    with es:
        K = Tracker(nc, es, dry)
        act, dve, pool, sp = K.act, K.dve, K.pool, K.sp

        def din(name, shape, dt=F32):
            return nc.dram_tensor(name, list(shape), dt, kind="ExternalInput").ap()

        xo = din("xo", [D, OWN])
        xc = din("xc", [D, OWN])
        memT = din("memT", [D, 256])
        Wd = {n: din(n, WSHAPES[n]) for n in WNAMES}
        gains = din("gains", [5, D])
        subln = din("subln", [256])
        hgnorm = din("hgnorm", [1024])
        hglb = din("hglb", [2, 1024])
        lamv = din("lamv", [4, 128])
        c_R = din("c_R", [128, 512])
        c_tri = din("c_tri", [128, 128])
        c_hgm = din("c_hgm", [128, 128])
        c_id = din("c_id", [128, 128])
        c_atab = din("c_atab", [128, 128])
        c_mask = din("c_mask", [128, 1])
        c_ebt = din("c_ebt", [128, 144])
        outT = nc.dram_tensor("outT", [D, OWN], F32, kind="ExternalOutput").ap()
        KS = nc.dram_tensor("ks_scr", [2 * NT, 128, 8, T], BF16, kind="Internal").ap()
        VS = nc.dram_tensor("vs_scr", [2 * NT, 128, 4, 4, 256], BF16, kind="Internal").ap()
        KSb = [Buf(f"ks{j}") for j in range(2 * NT)]
        VSb = [Buf(f"vs{j}") for j in range(2 * NT)]

        def sb(name, shape, dt):
            return es.enter_context(nc.sbuf_tensor(name, list(shape), dt))

        uid = [0]

        def sbt(name, shape, dt):
            uid[0] += 1
            return nc.sbuf_tensor(f"{name}_{uid[0]}", list(shape), dt)

        Wt = [sb(f"wslot{s}", [128, SLOT_ELEMS], BF16) for s in range(NSLOT)]
        Wb = [[Buf(f"wslot{s}lo"), Buf(f"wslot{s}hi")] for s in range(NSLOT)]
        H = sb("H", [128, NCH, T], F32)
        Hb = [Buf(f"H{c}") for c in range(NCH)]
        XN = sb("XN", [128, NCH, T], BF16)
        XNb = [Buf(f"XN{c}") for c in range(NCH)]
        RSTD = sb("RSTD", [128, T], F32)
        RSTDb = Buf("RSTD")
        SQ = [sb(f"SQ{i}", [128, T], BF16) for i in range(2)]
        SQb = [Buf(f"SQ{i}") for i in range(2)]
        SG = [sb(f"SG{i}", [128, T], F32) for i in range(2)]
        SGb = [Buf(f"SG{i}") for i in range(2)]
        ONES = sb("ONES", [128, 128], BF16)
        IDENT = sb("IDENT", [128, 128], BF16)
        CONb = Buf("consts")
        Rm = sb("Rm", [128, 512], F32)
        TRI = sb("TRI", [128, 128], F32)
        HGM = sb("HGM", [128, 128], F32)
        ATAB = sb("ATAB", [128, 128], F32)
        CB = sb("CB", [128, 128], F32)
        CM = sb("CM", [128, 1], F32)
        EBT = sb("EBT", [128, 144], F32)
        EBC = sb("EBC", [128, 144], F32)
        SCM = sb("SCM", [128, 512], BF16)
        G = sb("G", [128, 5, NCH], F32)
        SUBG = sb("SUBG", [128, 2], F32)
        HGN = sb("HGN", [128, 8], F32)
        LBL = sb("LBL", [128, 2, 8], F32)
        LB = sb("LB", [128, 8], F32)
        OML = sb("OML", [128, 8], F32)
        LQ = sb("LQ", [128, 4, 128], F32)
        LT = sb("LT", [128, 2, 128], F32)
        LS = sb("LS", [128, 2], F32)
        NLAM = sb("NLAM", [128, 1], F32)
        KX = sb("KX", [128, 8, 256], BF16)
        VX = sb("VX", [128, 2, 1024], BF16)
        KXb = Buf("KX")
        VXb = Buf("VX")
        ST = sb("ST", [128, 8, 128], F32)
        STb = [Buf(f"ST{h}") for h in range(8)]
        STbf = sb("STbf", [128, 8, 2, 128], BF16)
        STbfb = [[Buf(f"STbf{h}_{p}") for p in range(2)] for h in range(8)]
        PS = es.enter_context(nc.psum_tensor("PS", [128, 8, 512], F32))
        PB = [Buf(f"bank{i}") for i in range(8)]
        rr = {}

        def bank(setname, banks):
            i = rr.get(setname, 0)
            rr[setname] = i + 1
            b = banks[i % len(banks)]
            return PS[:, b, :], PB[b]

        GB = [0, 1, 2, 3]

        def gbank():
            return bank("g", GB)

        wstate = {"pos": 0, "issued": 0, "released": 0}
        HALF = SLOT_ELEMS // 2

        def w_src(name, k0, nk, c0, ncol):
            return Wd[name][k0 * 128:(k0 + nk) * 128, c0:c0 + ncol].rearrange("(k p) c -> p k c", p=128)

        def w_issue(i):
            d = plan[i]
            s = i % NSLOT
            if d[0] == "one":
                _, name, k0, nk, c0, ncol = d
                dst = Wt[s][:, 0:nk * ncol].rearrange("p (k c) -> p k c", c=ncol)
                K.dma(pool, dst, w_src(name, k0, nk, c0, ncol), writes=Wb[s], track=False)
            else:
                _, na, nka, c0a, nb_, nkb, c0b, ncol = d
                for hi, (nm, nk, c0) in enumerate(((na, nka, c0a), (nb_, nkb, c0b))):
                    dst = Wt[s][:, hi * HALF:hi * HALF + nk * ncol].rearrange("p (k c) -> p k c", c=ncol)
                    K.dma(pool, dst, w_src(nm, 0, nk, c0, ncol), writes=[Wb[s][hi]], track=False)

        def w_try_issue():
            while wstate["issued"] < len(plan) and wstate["issued"] - NSLOT < wstate["released"]:
                w_issue(wstate["issued"])
                wstate["issued"] += 1

        def w_req(desc):
            i = wstate["pos"]
            wstate["pos"] += 1
            if dry:
                plan.append(desc)
            else:
                assert plan[i] == desc, (i, plan[i], desc)
                w_try_issue()
                assert wstate["issued"] > i, "too many live weight pieces"
            return i % NSLOT

        def wrelease():
            if dry:
                return
            wstate["released"] += 1
            w_try_issue()

        def wpiece(name, k0, nk, c0, ncol):
            assert nk * ncol <= SLOT_ELEMS
            s = w_req(("one", name, k0, nk, c0, ncol))
            return Wt[s][:, 0:nk * ncol].rearrange("p (k c) -> p k c", c=ncol), Wb[s]

        def wpair(na, nka, c0a, nb_, nkb, c0b, ncol):
            assert nka * ncol <= HALF and nkb * ncol <= HALF
            s = w_req(("pair", na, nka, c0a, nb_, nkb, c0b, ncol))
            va = Wt[s][:, 0:nka * ncol].rearrange("p (k c) -> p k c", c=ncol)
            vb = Wt[s][:, HALF:HALF + nkb * ncol].rearrange("p (k c) -> p k c", c=ncol)
            return va, [Wb[s][0]], vb, [Wb[s][1]]

        ecnt = [0]

        def evac_copy(out, in_, reads, writes, scale=None):
            ecnt[0] += 1
            if scale is not None:
                K.op(act, lambda: nc.scalar.activation(out=out, in_=in_, func=AF.Copy, scale=scale), reads, writes)
            elif ecnt[0] % 2 == 0:
                K.op(act, lambda: nc.scalar.copy(out=out, in_=in_), reads, writes)
            else:
                K.op(dve, lambda: nc.vector.tensor_copy(out=out, in_=in_), reads, writes)

        def gemm_fm(wname, nk, c0, ncols, rhs_fn, N, evac, cb=512):
            cb = min(cb, ncols)
            for cblk in range(c0, c0 + ncols, cb):
                wv, wb = wpiece(wname, 0, nk, cblk, cb)
                for mi in range(cb // 128):
                    pap, pb = gbank()
                    for k in range(nk):
                        rap, rbufs = rhs_fn(k)
                        K.mm(pap[:, :N], lhsT=wv[:, k, mi * 128:(mi + 1) * 128], rhs=rap,
                             start=(k == 0), stop=(k == nk - 1), reads=wb + rbufs, pbuf=pb)
                    evac((cblk - c0) // 128 + mi, pap[:, :N], pb)
                wrelease()

        def gemm_tm(wname, nk, c0, ncols, lhs_fn, nsub, evac):
            for cblk in range(c0, c0 + ncols, 512):
                wv, wb = wpiece(wname, 0, nk, cblk, 512)
                for s in range(nsub):
                    pap, pb = gbank()
                    for k in range(nk):
                        lap, lbufs = lhs_fn(k, s)
                        K.mm(pap, lhsT=lap, rhs=wv[:, k, :], start=(k == 0), stop=(k == nk - 1),
                             reads=wb + lbufs, pbuf=pb)
                    evac(s, cblk - c0, pap, pb)
                wrelease()

        def rms_rstd(src_fn, nch, N, inv_n):
            pap, pb = bank("st", [7])
            for c in range(nch):
                sap, sbufs = src_fn(c)
                q = c % 2
                K.op(act, lambda: nc.scalar.activation(out=SQ[q][:, :N], in_=sap, func=AF.Square), sbufs, [SQb[q]])
                K.mm(pap[:, :N], lhsT=ONES[:], rhs=SQ[q][:, :N], start=(c == 0), stop=(c == nch - 1),
                     reads=[SQb[q], CONb], pbuf=pb)
            K.op(act, lambda: nc.scalar.activation(out=RSTD[:, :N], in_=pap[:, :N], func=AF.Sqrt, scale=inv_n, bias=EPSC[:, 0:1]),
                 [pb, CONb], [RSTDb])
            K.op(dve, lambda: nc.vector.reciprocal(out=RSTD[:, :N], in_=RSTD[:, :N]), [RSTDb], [RSTDb])

        EPSC = sb("EPSC", [128, 2], F32)
        K.op(dve, lambda: nc.vector.memset(ONES[:], 1.0), [], [CONb])
        K.op(dve, lambda: nc.vector.memset(EPSC[:, 0:1], EPS), [], [CONb])
        K.op(dve, lambda: nc.vector.memset(SCM[:], 1.0), [], [CONb])
        K.op(dve, lambda: nc.vector.memset(SCM[:].rearrange("p (c t) -> p c t", t=64)[:, :, 0:1], 0.0), [], [CONb])
        K.op(dve, lambda: nc.vector.memset(ST[:], 0.0), [], STb)
        K.op(dve, lambda: nc.vector.memset(STbf[:], 0.0), [], [b for hb in STbfb for b in hb])
        smallb = Buf("small")
        K.dma(pool, IDENT[:], c_id, writes=[smallb])
        for dst, src in ((Rm, c_R), (TRI, c_tri), (HGM, c_hgm), (ATAB, c_atab), (CM, c_mask), (EBT, c_ebt)):
            K.dma(sp, dst[:], src, writes=[smallb])
        with nc.allow_non_contiguous_dma(reason="tiny param layout loads"):
            for gi in range(5):
                K.dma(sp, G[:, gi, :], gains[gi].rearrange("(c p) -> p c", p=128), writes=[smallb])
            K.dma(sp, SUBG[:], subln.rearrange("(c p) -> p c", p=128), writes=[smallb])
            K.dma(sp, HGN[:], hgnorm.rearrange("(c p) -> p c", p=128), writes=[smallb])
            K.dma(sp, LBL[:], hglb.rearrange("r (c p) -> p r c", p=128), writes=[smallb])
        for r_ in range(4):
            K.dma(sp, LQ[:, r_, :], lamv[r_:r_ + 1, :].to_broadcast([128, 128]), writes=[smallb])
        sm = [smallb]
        K.op(dve, lambda: nc.vector.tensor_scalar(out=SUBG[:], in0=SUBG[:], scalar1=1.0 - LAMBDA_INIT, scalar2=None, op0=ALU.mult), sm, sm)
        K.op(dve, lambda: nc.vector.tensor_tensor(out=LB[:], in0=LBL[:, 0, :], in1=LBL[:, 1, :], op=ALU.subtract), sm, sm)
        K.op(act, lambda: nc.scalar.activation(out=LB[:], in_=LB[:], func=AF.Sigmoid), sm, sm)
        K.op(dve, lambda: nc.vector.tensor_scalar(out=OML[:], in0=LB[:], scalar1=-1.0, scalar2=1.0, op0=ALU.mult, op1=ALU.add), sm, sm)
        K.op(dve, lambda: nc.vector.tensor_tensor(out=LT[:, 0, :], in0=LQ[:, 0, :], in1=LQ[:, 1, :], op=ALU.mult), sm, sm)
        K.op(dve, lambda: nc.vector.tensor_tensor(out=LT[:, 1, :], in0=LQ[:, 2, :], in1=LQ[:, 3, :], op=ALU.mult), sm, sm)
        K.op(dve, lambda: nc.vector.reduce_sum(out=LS[:], in_=LT[:], axis=AX.X), sm, sm)
        K.op(act, lambda: nc.scalar.activation(out=LS[:], in_=LS[:], func=AF.Exp), sm, sm)
        K.op(dve, lambda: nc.vector.scalar_tensor_tensor(out=NLAM[:], in0=LS[:, 1:2], scalar=-LAMBDA_INIT, in1=LS[:, 0:1],
                                                          op0=ALU.add, op1=ALU.subtract), sm, sm)
        K.op(dve, lambda: nc.vector.tensor_scalar(out=CB[:], in0=ATAB[:], scalar1=CM[:, 0:1], scalar2=None, op0=ALU.add), sm, sm)
        K.op(dve, lambda: nc.vector.tensor_scalar(out=EBC[:], in0=EBT[:], scalar1=CM[:, 0:1], scalar2=None, op0=ALU.add), sm, sm)
        K.barrier()
        CONb.w = None
        CONb.r = {}
        if not dry:
            pass

        def consts_ready():
            return []

        with ExitStack() as ph:
            MT = ph.enter_context(sbt("MT", [128, NCH, 256], F32))
            MN = ph.enter_context(sbt("MN", [128, NCH, 256], BF16))
            MTb = Buf("MT")
            MNb = [Buf(f"MN{c}") for c in range(NCH)]
            K.dma(sp, MT[:], memT.rearrange("(c p) t -> p c t", p=128), writes=[MTb])
            rms_rstd(lambda c: (MT[:, c, :], [MTb]), NCH, 256, 1.0 / D)
            for c in range(NCH):
                K.op(dve, lambda: nc.vector.scalar_tensor_tensor(out=MN[:, c, :], in0=MT[:, c, :], scalar=G[:, 4, c:c + 1],
                                                                  in1=RSTD[:, :256], op0=ALU.mult, op1=ALU.mult),
                     [MTb, RSTDb], [MNb[c]])
            gemm_fm("w_mem_kv", NCH, 0, 1024, lambda k: (MN[:, k, :], [MNb[k]]), 256,
                    lambda m, p, pb: evac_copy(KX[:, m, :], p, [pb], [KXb]))
            gemm_tm("w_mem_kv", NCH, 1024, 1024, lambda k, s: (MN[:, k, s * 128:(s + 1) * 128], [MNb[k]]), 2,
                    lambda s, co, p, pb: evac_copy(VX[:, s, co:co + 512], p, [pb], [VXb]))
            K.barrier()

        def ffn(pre, gi):
            with ExitStack() as ph:
                HID = ph.enter_context(sbt("HID", [128, NHC, T], BF16))
                HIDb = [Buf(f"HID{j}") for j in range(NHC)]
                rms_rstd(lambda c: (H[:, c, :], [Hb[c]]), NCH, T, 1.0 / D)
                for c in range(NCH):
                    K.op(dve, lambda: nc.vector.scalar_tensor_tensor(out=XN[:, c, :], in0=H[:, c, :], scalar=G[:, gi, c:c + 1],
                                                                      in1=RSTD[:], op0=ALU.mult, op1=ALU.mult),
                         [Hb[c], RSTDb], [XNb[c]])
                for jb in range(NHC // 2):
                    wg, wgb, wu, wub = wpair(pre + "_wg", NCH, jb * 256, pre + "_wu", NCH, jb * 256, 256)
                    for mi in range(2):
                        j = jb * 2 + mi
                        pg, pgb = gbank()
                        pu, pub = gbank()
                        for k in range(NCH):
                            K.mm(pg, lhsT=wg[:, k, mi * 128:(mi + 1) * 128], rhs=XN[:, k, :], start=(k == 0),
                                 stop=(k == NCH - 1), reads=wgb + [XNb[k]], pbuf=pgb)
                        for k in range(NCH):
                            K.mm(pu, lhsT=wu[:, k, mi * 128:(mi + 1) * 128], rhs=XN[:, k, :], start=(k == 0),
                                 stop=(k == NCH - 1), reads=wub + [XNb[k]], pbuf=pub)
                        q = j % 2
                        K.op(act, lambda: nc.scalar.activation(out=SG[q][:], in_=pg, func=AF.Silu), [pgb], [SGb[q]])
                        K.op(dve, lambda: nc.vector.tensor_tensor(out=HID[:, j, :], in0=SG[q][:], in1=pu, op=ALU.mult),
                             [SGb[q], pub], [HIDb[j]])
                    wrelease()
                for mb2 in range(NCH // 2):
                    pbs = [gbank(), gbank()]
                    for half in range(2):
                        wd, wdb = wpiece(pre + "_wd", half * 22, 22, mb2 * 256, 256)
                        for mi in range(2):
                            p, pb = pbs[mi]
                            for jj in range(22):
                                j = half * 22 + jj
                                K.mm(p, lhsT=wd[:, jj, mi * 128:(mi + 1) * 128], rhs=HID[:, j, :], start=(j == 0), stop=(j == NHC - 1),
                                     reads=wdb + [HIDb[j]], pbuf=pb, tick=(jj == 21))
                        wrelease()
                    for mi in range(2):
                        m = mb2 * 2 + mi
                        p, pb = pbs[mi]
                        K.op(dve, lambda: nc.vector.scalar_tensor_tensor(out=H[:, m, :], in0=p, scalar=0.5, in1=H[:, m, :],
                                                                          op0=ALU.mult, op1=ALU.add), [pb, Hb[m]], [Hb[m]])
                K.barrier()

        def load_x(src, t0):
            K.dma(sp, H[:], src.rearrange("(c p) t -> p c t", p=128)[:, :, t0:t0 + T], writes=Hb)

        def mix_norm():
            rms_rstd(lambda c: (H[:, c, :], [Hb[c]]), NCH, T, 1.0 / D)
            for c in range(NCH):
                K.op(dve, lambda: nc.vector.scalar_tensor_tensor(out=XN[:, c, :], in0=H[:, c, :], scalar=G[:, 1, c:c + 1],
                                                                  in1=RSTD[:], op0=ALU.mult, op1=ALU.mult),
                     [Hb[c], RSTDb], [XNb[c]])

        xn_rhs = lambda k: (XN[:, k, :], [XNb[k]])
        xn_lhs = lambda k, s: (XN[:, k, s * 128:(s + 1) * 128], [XNb[k]])

        def da_kv(jt):
            with ExitStack() as ph:
                KTs = ph.enter_context(sbt("KTs", [128, 8, T], BF16))
                VTs = ph.enter_context(sbt("VTs", [128, 4, 4, 256], BF16))
                KTb = [Buf(f"KTs{m}") for m in range(8)]
                VTb = [Buf(f"VTs{s}") for s in range(8)]
                gemm_fm("w_in", NCH, 1024, 1024, xn_rhs, T,
                        lambda m, p, pb: evac_copy(KTs[:, m, :], p, [pb], [KTb[m]]))
                gemm_tm("w_in", NCH, 2048, 1024, xn_lhs, 4,
                        lambda s, co, p, pb: evac_copy(VTs[:, co // 256:co // 256 + 2, s, :], p.rearrange("p (h c) -> p h c", c=256), [pb], [VTb[s * 2 + co // 512]]))
                K.dma(sp, KS[jt], KTs[:], reads=KTb, writes=[KSb[jt]])
                K.dma(sp, VS[jt], VTs[:], reads=VTb, writes=[VSb[jt]])
                K.barrier()

        def hg(own, Y):
            for Gp in range(2):
                with ExitStack() as ph:
                    def t_(name, shape, dt, st=ph):
                        return st.enter_context(sbt(name, shape, dt))
                    IH = t_("hgIH", [128, 4, 512], BF16)
                    KH = t_("hgKH", [128, 4, T], BF16)
                    Et = t_("hgEt", [128, 4, 8], F32)
                    IHb, KHb, Etb = [Buf(f"IH{s}") for s in range(4)], Buf("KH"), Buf("Et")
                    if own:
                        KTb_ = t_("hgKTb", [128, 4, T], BF16)
                        QTb_ = t_("hgQTb", [128, 4, T], BF16)
                        KTbb, QTbb = Buf("KTb"), Buf("QTb")
                    with ExitStack() as ph1:
                        A1 = t_("hgA1", [128, 4, T], F32, ph1)
                        A2 = t_("hgA2", [128, 4, T], F32, ph1)
                        A3 = t_("hgA3", [128, 4, T], F32, ph1)
                        A1b = [Buf(f"A1_{h}") for h in range(4)]
                        A2b, A3b = Buf("A2"), Buf("A3")
                        if own:
                            QH = t_("hgQH", [128, 4, T], F32, ph1)
                            QHb = [Buf(f"QH{h}") for h in range(4)]
                            gemm_fm("w_in", NCH, 3072 + Gp * 512, 512, xn_rhs, T,
                                    lambda m, p, pb: K.op(act, lambda: nc.scalar.activation(out=QH[:, m, :], in_=p, func=AF.Silu), [pb], [QHb[m]]))
                        gemm_fm("w_in", NCH, 4096 + Gp * 512, 512, xn_rhs, T,
                                lambda m, p, pb: K.op(act, lambda: nc.scalar.activation(out=A1[:, m, :], in_=p, func=AF.Sigmoid), [pb], [A1b[m]]))
                        gemm_tm("w_in", NCH, 5120 + Gp * 512, 512, xn_lhs, 4,
                                lambda s, co, p, pb: evac_copy(IH[:, s, :], p, [pb], [IHb[s]]))
                        for hh in range(4):
                            h = Gp * 4 + hh
                            K.op(dve, lambda: nc.vector.tensor_scalar(out=A1[:, hh, :], in0=A1[:, hh, :], scalar1=OML[:, h:h + 1],
                                                                      scalar2=LB[:, h:h + 1], op0=ALU.mult, op1=ALU.add), [A1b[hh]], [A1b[hh]])
                        K.op(act, lambda: nc.scalar.activation(out=A2[:], in_=A1[:], func=AF.Ln), A1b, [A2b])
                        K.op(dve, lambda: nc.vector.tensor_scalar(out=A1[:], in0=A1[:], scalar1=-1.0, scalar2=1.0, op0=ALU.mult, op1=ALU.add), A1b, A1b)
                        for hh in range(4):
                            K.op(dve, lambda: nc.vector.tensor_tensor_scan(out=A3[:, hh, :], data0=SCM[:], data1=A2[:, hh, :], initial=0.0,
                                                                           op0=ALU.mult, op1=ALU.add), [A2b], [A3b])
                        A3v = A3[:].rearrange("p h (c t) -> p h c t", t=64)
                        K.op(act, lambda: nc.scalar.activation(out=Et[:], in_=A3v[:, :, :, 63], func=AF.Exp), [A3b], [Etb])
                        K.op(act, lambda: nc.scalar.activation(out=A2[:], in_=A3[:], func=AF.Exp, scale=-1.0), [A3b, A2b], [A2b])
                        K.op(dve, lambda: nc.vector.tensor_tensor(out=A2[:], in0=A2[:], in1=A1[:], op=ALU.mult), [A2b] + A1b, [A2b])
                        K.op(dve, lambda: nc.vector.tensor_tensor(out=KH[:].rearrange("p h (c t) -> p (h c) t", t=64),
                                                                  in0=A2[:].rearrange("p h (c t) -> p (h c) t", t=64),
                                                                  in1=Et[:].rearrange("p h c -> p (h c)").unsqueeze(2).to_broadcast([128, 32, 64]),
                                                                  op=ALU.mult), [A2b, Etb], [KHb])
                        if own:
                            K.op(act, lambda: nc.scalar.copy(out=KTb_[:], in_=A2[:]), [A2b], [KTbb])
                            K.op(act, lambda: nc.scalar.activation(out=A3[:], in_=A3[:], func=AF.Exp), [A3b], [A3b])
                            K.op(dve, lambda: nc.vector.tensor_tensor(out=QTb_[:], in0=QH[:], in1=A3[:], op=ALU.mult), QHb + [A3b], [QTbb])
                        K.barrier()
                    with ExitStack() as ph2:
                        KHt = t_("hgKHt", [128, 4, 4, 128], BF16, ph2)
                        KHtb = [Buf(f"KHt{h}") for h in range(4)]
                        if own:
                            GH = t_("hgGH", [128, 4, T], BF16, ph2)
                            AT = [t_(f"hgAT{i_}", [128, 128], BF16, ph2) for i_ in range(4)]
                            OH = t_("hgOH", [128, 4, T], F32, ph2)
                            GHb = [Buf(f"GH{h}") for h in range(4)]
                            ATb = [Buf(f"AT{i_}") for i_ in range(4)]
                            OHb = [Buf(f"OH{h}") for h in range(4)]
                            gemm_fm("w_in", NCH, 6144 + Gp * 512, 512, xn_rhs, T,
                                    lambda m, p, pb: K.op(act, lambda: nc.scalar.activation(out=GH[:, m, :], in_=p, func=AF.Sigmoid), [pb], [GHb[m]]))
                        for hh in range(4):
                            pap, pb = gbank()
                            for s in range(4):
                                K.mm(pap[:, s * 128:(s + 1) * 128], lhsT=KH[:, hh, s * 128:(s + 1) * 128], rhs=IDENT[:],
                                     start=True, stop=True, reads=[KHb], pbuf=pb)
                            evac_copy(KHt[:, hh, :, :], pap.rearrange("p (s k) -> p s k", k=128), [pb], [KHtb[hh]])
                        OB = [0, 1, 2, 3]
                        for s in range(4):
                            if own:
                                for hh in range(4):
                                    sp_, spb = bank("hs", [4, 5])
                                    K.mm(sp_[:, 0:128], lhsT=KTb_[:, hh, s * 128:(s + 1) * 128], rhs=QTb_[:, hh, s * 128:(s + 1) * 128],
                                         start=True, stop=True, reads=[KTbb, QTbb], pbuf=spb)
                                    K.op(dve, lambda: nc.vector.tensor_tensor(out=AT[hh][:], in0=sp_[:, 0:128], in1=HGM[:], op=ALU.mult), [spb], [ATb[hh]])
                                for hh in range(4):
                                    K.mm(PS[:, OB[hh], 0:128], lhsT=IH[:, s, hh * 128:(hh + 1) * 128], rhs=AT[hh][:], start=True, stop=False,
                                         reads=[IHb[s], ATb[hh]], pbuf=PB[OB[hh]])
                            for c2 in range(2):
                                ch = s * 2 + c2
                                par = ch % 2
                                if own:
                                    for hh in range(4):
                                        h = Gp * 4 + hh
                                        K.mm(PS[:, OB[hh], c2 * 64:(c2 + 1) * 64], lhsT=STbf[:, h, par, :],
                                             rhs=QTb_[:, hh, s * 128 + c2 * 64:s * 128 + (c2 + 1) * 64], start=False, stop=(c2 == 1),
                                             reads=[STbfb[h][par], QTbb], pbuf=PB[OB[hh]], tick=True)
                                for hh in range(4):
                                    h = Gp * 4 + hh
                                    pp, ppb = bank("hp", [6, 7] if own else [4, 5, 6, 7])
                                    K.mm(pp[:, 0:128], lhsT=KHt[c2 * 64:(c2 + 1) * 64, hh, s, :], rhs=IH[c2 * 64:(c2 + 1) * 64, s, hh * 128:(hh + 1) * 128],
                                         start=True, stop=True, reads=[KHtb[hh], IHb[s]], pbuf=ppb)
                                    K.op(dve, lambda: nc.vector.scalar_tensor_tensor(out=ST[:, h, :], in0=ST[:, h, :], scalar=Et[:, hh, ch:ch + 1],
                                                                                      in1=pp[:, 0:128], op0=ALU.mult, op1=ALU.add),
                                         [STb[h], Etb, ppb], [STb[h]])
                                    K.op(act, lambda: nc.scalar.copy(out=STbf[:, h, 1 - par, :], in_=ST[:, h, :]), [STb[h]], [STbfb[h][1 - par]])
                            if own:
                                for hh in range(4):
                                    evac_copy(OH[:, hh, s * 128:(s + 1) * 128], PS[:, OB[hh], 0:128], [PB[OB[hh]]], [OHb[hh]])
                        if own:
                            SQ4 = KTb_
                            K.op(act, lambda: nc.scalar.activation(out=SQ4[:], in_=OH[:], func=AF.Square), OHb + [KTbb], [KTbb])
                            for hh in range(4):
                                h = Gp * 4 + hh
                                pap, pb = gbank()
                                K.mm(pap, lhsT=ONES[:], rhs=SQ4[:, hh, :], start=True, stop=True, reads=[KTbb], pbuf=pb)
                                q = hh % 2
                                K.op(act, lambda: nc.scalar.activation(out=SG[q][:], in_=pap, func=AF.Sqrt, scale=1.0 / 128, bias=EPSC[:, 0:1]), [pb], [SGb[q]])
                                K.op(dve, lambda: nc.vector.reciprocal(out=SG[q][:], in_=SG[q][:]), [SGb[q]], [SGb[q]])
                                K.op(dve, lambda: nc.vector.tensor_tensor(out=OH[:, hh, :], in0=OH[:, hh, :], in1=SG[q][:], op=ALU.mult), [OHb[hh], SGb[q]], [OHb[hh]])
                                K.op(dve, lambda: nc.vector.scalar_tensor_tensor(out=Y[0][:, 8 + h, :], in0=OH[:, hh, :], scalar=HGN[:, h:h + 1], in1=GH[:, hh, :],
                                                                                  op0=ALU.mult, op1=ALU.mult), [OHb[hh], GHb[hh]], [Y[1][8 + h]])
                        K.barrier()

        def da_attn(i, Y):
            nkt = NT + i + 1
            with ExitStack() as ph:
                def t_(name, shape, dt):
                    return ph.enter_context(sbt(name, shape, dt))
                QT = t_("daQT", [128, 8, T], BF16)
                QTb = [Buf(f"daQT{m}") for m in range(8)]
                gemm_fm("w_in", NCH, 0, 1024, xn_rhs, T,
                        lambda m, p, pb: evac_copy(QT[:, m, :], p, [pb], [QTb[m]], scale=128 ** -0.5))
                VBh = t_("daVB", [128, 2 * NT, 4, 256], BF16)
                KB = t_("daKB0", [128, 2 * NT, T], BF16)
                TT = [t_(f"daTT{i_}", [128, T], F32) for i_ in range(2)] + [SG[0], SG[1]]
                PT = [t_(f"daPT{i_}", [128, T], BF16) for i_ in range(4)]
                RD = RSTD
                O0 = t_("daO0", [128, 2, T], F32)
                YD = t_("daYD", [128, 2, T], F32)
                VBb, KBb = Buf("VBh"), Buf("KB0")
                TTb = [Buf("TT0"), Buf("TT1"), SGb[0], SGb[1]]
                PTb = [Buf(f"PT{i_}") for i_ in range(4)]
                RDb, O0b, YDb = RSTDb, [Buf("O0_0"), Buf("O0_1")], [Buf("YD0"), Buf("YD1")]
                all_blocks = [(j, kb) for j in range(nkt) for kb in range(4)]

                def bdist(j, kb):
                    return (j - (NT + i)) * 512 + kb * 128

                MIN_DIST = {0: -512, 1: -1792}
                DEPTH = 3
                SB = [0, 1, 2, 3, 4]
                accs = [5, 6, 7]
                for h in range(4):
                    K.dma(sp, VBh[:, 0:nkt, :, :].rearrange("p j s c -> p j (s c)"), VS[0:nkt, :, h, :, :].rearrange("j p s c -> p j (s c)"),
                          reads=VSb[:nkt], writes=[VBb])
                    blocks = [(j, kb) for (j, kb) in all_blocks if bdist(j, kb) >= MIN_DIST.get(h, -10 ** 9)]
                    nb = len(blocks)
                    for c in range(2):
                        hc = h * 2 + c
                        K.dma(sp, KB[:, 0:nkt, :], KS[0:nkt, :, hc, :].rearrange("j p t -> p j t"), reads=KSb[:nkt], writes=[KBb])

                        def geom(bi):
                            j, kb = blocks[bi]
                            diag = (j == nkt - 1)
                            qs = kb * 128 if diag else 0
                            return j, kb, diag, qs, T - qs

                        def s_stage(bi):
                            j, kb, diag, qs, n = geom(bi)
                            sp_, spb = bank("das", SB)
                            K.mm(sp_[:, :n], lhsT=KB[:, j, kb * 128:(kb + 1) * 128], rhs=QT[:, hc, qs:], start=True, stop=True,
                                 reads=[KBb, QTb[hc]], pbuf=spb)
                            return sp_, spb

                        sq_ = {}
                        for bi in range(min(DEPTH, nb)):
                            sq_[bi] = s_stage(bi)
                        for bi in range(nb):
                            if bi + DEPTH < nb:
                                sq_[bi + DEPTH] = s_stage(bi + DEPTH)
                            sp_, spb = sq_.pop(bi)
                            j, kb, diag, qs, n = geom(bi)
                            dist = bdist(j, kb)
                            q = bi % 4
                            if h == 0:
                                K.op(dve, lambda: nc.vector.scalar_tensor_tensor(out=TT[q][:, :n], in0=Rm[:, :n], scalar=SLOPES[h], in1=sp_[:, :n],
                                                                                  op0=ALU.mult, op1=ALU.add), [spb], [TTb[q]])
                                if diag:
                                    K.op(dve, lambda: nc.vector.tensor_tensor(out=TT[q][:, 0:128], in0=TT[q][:, 0:128], in1=TRI[:], op=ALU.add), [TTb[q]], [TTb[q]])
                                    bias = 0.0
                                elif j < NT:
                                    nidx = (-dist) // 128
                                    bias = CB[:, h * 32 + nidx:h * 32 + nidx + 1]
                                else:
                                    bias = float(SLOPES[h] * dist)
                                K.op(act, lambda: nc.scalar.activation(out=PT[q][:, :n], in_=TT[q][:, :n], func=AF.Exp, bias=bias), [TTb[q]], [PTb[q]])
                            else:
                                col = h * 36 + 32 + dist // 128
                                bcol = (EBC if j < NT else EBT)[:, col:col + 1]
                                if diag:
                                    K.op(dve, lambda: nc.vector.tensor_tensor(out=TT[q][:, 0:128], in0=sp_[:, 0:128], in1=TRI[:], op=ALU.add), [spb], [TTb[q]])
                                    K.op(act, lambda: nc.scalar.activation(out=PT[q][:, 0:128], in_=TT[q][:, 0:128], func=AF.Exp, bias=bcol), [TTb[q]], [PTb[q]])
                                    if n > 128:
                                        K.op(act, lambda: nc.scalar.activation(out=PT[q][:, 128:n], in_=sp_[:, 128:n], func=AF.Exp, bias=bcol), [spb], [PTb[q]])
                                else:
                                    K.op(act, lambda: nc.scalar.activation(out=PT[q][:, :n], in_=sp_[:, :n], func=AF.Exp, bias=bcol), [spb], [PTb[q]])
                            last = (bi == nb - 1)
                            for e in range(2):
                                K.mm(PS[:, accs[e], qs:], lhsT=VBh[:, j, kb, e * 128:(e + 1) * 128], rhs=PT[q][:, :n], start=(bi == 0), stop=last,
                                     reads=[VBb, PTb[q]], pbuf=PB[accs[e]])
                            K.mm(PS[:, accs[2], qs:], lhsT=ONES[:], rhs=PT[q][:, :n], start=(bi == 0), stop=last, reads=[PTb[q]], pbuf=PB[accs[2]])
                        K.op(dve, lambda: nc.vector.reciprocal(out=RD[:], in_=PS[:, accs[2], :]), [PB[accs[2]]], [RDb])
                        for e in range(2):
                            if c == 0:
                                K.op(dve, lambda: nc.vector.tensor_tensor(out=O0[:, e, :], in0=PS[:, accs[e], :], in1=RD[:], op=ALU.mult),
                                     [PB[accs[e]], RDb], [O0b[e]])
                            else:
                                K.op(dve, lambda: nc.vector.tensor_tensor(out=YD[:, e, :], in0=PS[:, accs[e], :], in1=RD[:], op=ALU.mult),
                                     [PB[accs[e]], RDb], [YDb[e]])
                                K.op(dve, lambda: nc.vector.scalar_tensor_tensor(out=YD[:, e, :], in0=YD[:, e, :], scalar=NLAM[:, 0:1], in1=O0[:, e, :],
                                                                                  op0=ALU.mult, op1=ALU.add), [YDb[e], O0b[e]], [YDb[e]])
                        if c == 1:
                            pap, pb = bank("das", SB)
                            for e in range(2):
                                K.op(act, lambda: nc.scalar.activation(out=SQ[e][:], in_=YD[:, e, :], func=AF.Square), [YDb[e]], [SQb[e]])
                                K.mm(pap, lhsT=ONES[:], rhs=SQ[e][:], start=(e == 0), stop=(e == 1), reads=[SQb[e]], pbuf=pb)
                            K.op(act, lambda: nc.scalar.activation(out=RD[:], in_=pap, func=AF.Sqrt, scale=1.0 / 256, bias=EPSC[:, 0:1]), [pb], [RDb])
                            K.op(dve, lambda: nc.vector.reciprocal(out=RD[:], in_=RD[:]), [RDb], [RDb])
                            for e in range(2):
                                K.op(dve, lambda: nc.vector.scalar_tensor_tensor(out=Y[0][:, h * 2 + e, :], in0=YD[:, e, :], scalar=SUBG[:, e:e + 1], in1=RD[:],
                                                                                  op0=ALU.mult, op1=ALU.mult), [YDb[e], RDb], [Y[1][h * 2 + e]])
                K.barrier()

        def xa(Y):
            with ExitStack() as ph:
                QX = ph.enter_context(sbt("xaQX", [128, 8, T], BF16))
                PT = [ph.enter_context(sbt(f"xaPT{i_}", [128, T], BF16)) for i_ in range(2)]
                RD = ph.enter_context(sbt("xaRD", [128, T], F32))
                QXb = [Buf(f"QX{m}") for m in range(8)]
                PTb = [Buf("xPT0"), Buf("xPT1")]
                RDb = Buf("xRD")
                gemm_fm("w_in", NCH, 7168, 1024, xn_rhs, T,
                        lambda m, p, pb: evac_copy(QX[:, m, :], p, [pb], [QXb[m]], scale=256 ** -0.5))
                xblocks = [(h, mb) for h in range(4) for mb in range(2)]
                SB = [0, 1, 2, 3, 4]
                accs = [5, 6, 7]

                def xs_stage(bi):
                    h, mb = xblocks[bi]
                    sp_, spb = bank("das", SB)
                    for dc in range(2):
                        K.mm(sp_, lhsT=KX[:, h * 2 + dc, mb * 128:(mb + 1) * 128], rhs=QX[:, h * 2 + dc, :], start=(dc == 0), stop=(dc == 1),
                             reads=[KXb, QXb[h * 2 + dc]], pbuf=spb)
                    return sp_, spb

                xq = {0: xs_stage(0), 1: xs_stage(1)}
                for bi, (h, mb) in enumerate(xblocks):
                    if bi + 2 < len(xblocks):
                        xq[bi + 2] = xs_stage(bi + 2)
                    sp_, spb = xq.pop(bi)
                    K.op(act, lambda: nc.scalar.activation(out=PT[mb][:], in_=sp_, func=AF.Exp), [spb], [PTb[mb]])
                    for e in range(2):
                        K.mm(PS[:, accs[e], :], lhsT=VX[:, mb, h * 256 + e * 128:h * 256 + (e + 1) * 128], rhs=PT[mb][:], start=(mb == 0), stop=(mb == 1),
                             reads=[VXb, PTb[mb]], pbuf=PB[accs[e]])
                    K.mm(PS[:, accs[2], :], lhsT=ONES[:], rhs=PT[mb][:], start=(mb == 0), stop=(mb == 1), reads=[PTb[mb]], pbuf=PB[accs[2]])
                    if mb == 1:
                        K.op(dve, lambda: nc.vector.reciprocal(out=RD[:], in_=PS[:, accs[2], :]), [PB[accs[2]]], [RDb])
                        for e in range(2):
                            K.op(dve, lambda: nc.vector.tensor_tensor(out=Y[0][:, 16 + h * 2 + e, :], in0=PS[:, accs[e], :], in1=RD[:], op=ALU.mult),
                                 [PB[accs[e]], RDb], [Y[1][16 + h * 2 + e]])
                K.barrier()

        def merge_out(Y):
            with ExitStack() as ph:
                MACC = ph.enter_context(sbt("MACC", [128, 4, T], F32))
                MRG = ph.enter_context(sbt("MRG", [128, NCH, T], BF16))
                MACCb = [Buf(f"MACC{m}") for m in range(4)]
                MRGb = [Buf(f"MRG{m}") for m in range(NCH)]
                for mb2 in range(8):
                    for b in range(3):
                        gw, gwb, bw, bwb = wpair("w_in", NCH, 8192 + b * 2048 + mb2 * 256, f"wb{b}", 8, mb2 * 256, 256)
                        for mi in range(2):
                            m = mb2 * 2 + mi
                            pg, pgb = gbank()
                            pr, prb = gbank()
                            for k in range(NCH):
                                K.mm(pg, lhsT=gw[:, k, mi * 128:(mi + 1) * 128], rhs=XN[:, k, :], start=(k == 0), stop=(k == NCH - 1),
                                     reads=gwb + [XNb[k]], pbuf=pgb)
                            for k in range(8):
                                K.mm(pr, lhsT=bw[:, k, mi * 128:(mi + 1) * 128], rhs=Y[0][:, b * 8 + k, :], start=(k == 0), stop=(k == 7),
                                     reads=bwb + [Y[1][b * 8 + k]], pbuf=prb)
                            q = (b * 2 + mi) % 2
                            K.op(act, lambda: nc.scalar.activation(out=SG[q][:], in_=pg, func=AF.Sigmoid), [pgb], [SGb[q]])
                            if b == 0:
                                K.op(dve, lambda: nc.vector.tensor_tensor(out=MACC[:, mi, :], in0=SG[q][:], in1=pr, op=ALU.mult), [SGb[q], prb], [MACCb[mi]])
                            else:
                                K.op(dve, lambda: nc.vector.tensor_tensor(out=SG[q][:], in0=SG[q][:], in1=pr, op=ALU.mult), [SGb[q], prb], [SGb[q]])
                                if b == 1:
                                    K.op(dve, lambda: nc.vector.tensor_tensor(out=MACC[:, mi, :], in0=MACC[:, mi, :], in1=SG[q][:], op=ALU.add),
                                         [SGb[q], MACCb[mi]], [MACCb[mi]])
                                else:
                                    K.op(dve, lambda: nc.vector.tensor_tensor(out=MRG[:, m, :], in0=MACC[:, mi, :], in1=SG[q][:], op=ALU.add),
                                         [SGb[q], MACCb[mi]], [MRGb[m]])
                        wrelease()
                gemm_fm("w_out", NCH, 0, D, lambda k: (MRG[:, k, :], [MRGb[k]]), T,
                        lambda m, p, pb: K.op(dve, lambda: nc.vector.tensor_tensor(out=H[:, m, :], in0=p, in1=H[:, m, :], op=ALU.add), [pb, Hb[m]], [Hb[m]]))
                K.barrier()

        OUTC = [sb(f"OUTC{i_}", [128, T], F32) for i_ in range(2)]
        OUTCb = [Buf("OUTC0"), Buf("OUTC1")]

        def final_out(t0, gi=3, do_norm=True):
            if do_norm:
                rms_rstd(lambda c: (H[:, c, :], [Hb[c]]), NCH, T, 1.0 / D)
            ov = outT.rearrange("(c p) t -> p c t", p=128)
            for c in range(NCH):
                q = c % 2
                if do_norm:
                    K.op(dve, lambda: nc.vector.scalar_tensor_tensor(out=OUTC[q][:], in0=H[:, c, :], scalar=G[:, gi, c:c + 1], in1=RSTD[:],
                                                                      op0=ALU.mult, op1=ALU.mult), [Hb[c], RSTDb], [OUTCb[q]])
                else:
                    K.op(dve, lambda: nc.vector.tensor_copy(out=OUTC[q][:], in_=H[:, c, :]), [Hb[c]], [OUTCb[q]])
                K.dma(sp, ov[:, c, t0:t0 + T], OUTC[q][:], reads=[OUTCb[q]])
            K.barrier()

        stage = debug_stage
        if stage == "ffn1":
            for i in range(NT):
                load_x(xo, i * T)
                ffn("ffn1", 0)
                final_out(i * T, do_norm=False)
        else:
            for j in range(NT):
                load_x(xc, j * T)
                ffn("ffn1", 0)
                mix_norm()
                da_kv(j)
                hg(False, None)
            for i in range(NT):
                load_x(xo, i * T)
                ffn("ffn1", 0)
                mix_norm()
                da_kv(NT + i)
                with ExitStack() as yph:
                    Yt = yph.enter_context(sbt("Y", [128, 24, T], BF16))
                    Y = (Yt, [Buf(f"Y{m}") for m in range(24)])
                    da_attn(i, Y)
                    hg(True, Y)
                    xa(Y)
                    merge_out(Y)
                ffn("ffn2", 2)
                final_out(i * T)
        if not dry:
            for ev in list(K.sp_dma.values()):
                sp.wait(ev)
            for b in OUTCb:
                for ev in b.r.values():
                    sp.wait(ev)
        build.stats = (K.n_inst, K.nsem)
    return nc


_CONST_CACHE = {}


def _consts():
    if _CONST_CACHE:
        return _CONST_CACHE
    ki = np.arange(128, dtype=np.float32)[:, None]
    qi = np.arange(512, dtype=np.float32)[None, :]
    R = (ki - qi).astype(np.float32)
    tri = np.where(ki <= qi[:, :128], 0.0, NEG).astype(np.float32)
    s_ = np.arange(128)[:, None]
    t_ = np.arange(128)[None, :]
    hgm = ((s_ // 64 == t_ // 64) & (s_ <= t_)).astype(np.float32)
    ident = np.eye(128, dtype=np.float32)
    atab = np.zeros((128, 128), np.float32)
    for h in range(4):
        for n in range(32):
            atab[:, h * 32 + n] = -SLOPES[h] * 128.0 * n
    ebt = np.zeros((128, 144), np.float32)
    for h in range(4):
        for d in range(36):
            ebt[:, h * 36 + d] = SLOPES[h] * (np.arange(128, dtype=np.float32) + (d - 32) * 128.0)
    _CONST_CACHE.update(c_R=R, c_tri=tri, c_hgm=hgm, c_id=ident, c_atab=atab, c_ebt=ebt)
    return _CONST_CACHE


def _in_maps(inputs):
    x = np.asarray(inputs["x"], np.float32)
    mem = np.asarray(inputs["mem"], np.float32)
    w = {
        "ffn1_wg": inputs["ffn1_w_gate"][0], "ffn1_wu": inputs["ffn1_w_up"][0], "ffn1_wd": inputs["ffn1_w_down"][0],
        "w_in": inputs["w_in"][0], "w_mem_kv": inputs["w_mem_kv"][0], "wb0": inputs["w_branch_da"][0],
        "wb1": inputs["w_branch_hg"][0], "wb2": inputs["w_branch_xa"][0], "w_out": inputs["w_out"][0],
        "ffn2_wg": inputs["ffn2_w_gate"][0], "ffn2_wu": inputs["ffn2_w_up"][0], "ffn2_wd": inputs["ffn2_w_down"][0],
    }
    w = {k: np.ascontiguousarray(np.asarray(v, np.float32)) for k, v in w.items()}
    gains = np.ascontiguousarray(np.stack([np.asarray(inputs["ffn1_norm"][0]), np.asarray(inputs["mix_norm"][0]),
                                           np.asarray(inputs["ffn2_norm"][0]), np.asarray(inputs["final_norm"]),
                                           np.asarray(inputs["mem_norm"][0])]).astype(np.float32))
    lamv = np.ascontiguousarray(np.stack([np.asarray(inputs["da_lambda_q1"][0]), np.asarray(inputs["da_lambda_k1"][0]),
                                          np.asarray(inputs["da_lambda_q2"][0]), np.asarray(inputs["da_lambda_k2"][0])]).astype(np.float32))
    common = dict(w)
    common.update(gains=gains, subln=np.ascontiguousarray(np.asarray(inputs["da_subln"][0], np.float32)),
                  hgnorm=np.ascontiguousarray(np.asarray(inputs["hg_norm"][0], np.float32)),
                  hglb=np.ascontiguousarray(np.asarray(inputs["hg_lb_logits"], np.float32)), lamv=lamv)
    common.update(_consts())
    maps = []
    for c in range(8):
        b, half = c // 2, c % 2
        m = dict(common)
        m["xo"] = np.ascontiguousarray(x[b, half * OWN:(half + 1) * OWN, :].T)
        if half == 1:
            m["xc"] = np.ascontiguousarray(x[b, 0:OWN, :].T)
            m["c_mask"] = np.zeros((128, 1), np.float32)
        else:
            m["xc"] = np.zeros((D, OWN), np.float32)
            m["c_mask"] = np.full((128, 1), NEG, np.float32)
        m["memT"] = np.ascontiguousarray(mem[b].T)
        maps.append(m)
    return maps


_NC_CACHE = {}


def _get_nc(stage=None):
    if stage not in _NC_CACHE:
        plan = []
        build(True, plan, stage)
        _NC_CACHE[stage] = build(False, plan, stage)
    return _NC_CACHE[stage]


def kernel(**inputs):
    nc = _get_nc(None)
    maps = _in_maps(inputs)
    res = run_bass_kernel_spmd(nc, maps, core_ids=list(range(8)))
    out = np.empty((4, 4096, D), np.float32)
    for c in range(8):
        b, half = c // 2, c % 2
        out[b, half * OWN:(half + 1) * OWN, :] = np.asarray(res.results[c]["outT"]).T
    return out
```

```python
import numpy as np
import concourse.bass as bass
import concourse.mybir as mybir
from concourse.bass_utils import run_bass_kernel_spmd
from contextlib import ExitStack

F32 = mybir.dt.float32
BF16 = mybir.dt.bfloat16
AF = mybir.ActivationFunctionType
ALU = mybir.AluOpType
AX = mybir.AxisListType

D = 2048
DFF = 5632
NCH = 16
NHC = 44
T = 512
OWN = 2048
NT = OWN // T
EPS = 1e-6
LAMBDA_INIT = 0.2
SLOPES = [2.0 ** (-8.0 * (i + 1) / 4) for i in range(4)]
NEG = -30000.0
NSLOT = 3
SLOT_ELEMS = 8192

WNAMES = ["ffn1_wg", "ffn1_wu", "ffn1_wd", "w_in", "w_mem_kv", "wb0", "wb1", "wb2", "w_out",
          "ffn2_wg", "ffn2_wu", "ffn2_wd"]
WSHAPES = {"ffn1_wg": (D, DFF), "ffn1_wu": (D, DFF), "ffn1_wd": (DFF, D), "w_in": (D, 14336),
           "w_mem_kv": (D, 2048), "wb0": (1024, D), "wb1": (1024, D), "wb2": (1024, D), "w_out": (D, D),
           "ffn2_wg": (D, DFF), "ffn2_wu": (D, DFF), "ffn2_wd": (DFF, D)}


class Sem:
    __slots__ = ("h", "id")

    def __init__(self, h, i):
        self.h = h
        self.id = i


class Buf:
    __slots__ = ("name", "w", "r", "sem_in", "sem_out", "cnt_in", "cnt_out")

    def __init__(self, name):
        self.name = name
        self.w = None
        self.r = {}
        self.sem_in = None
        self.sem_out = None
        self.cnt_in = 0
        self.cnt_out = 0


class Eng:
    def __init__(self, K, name, h):
        self.K = K
        self.name = name
        self.h = h
        self.sem = None
        self.cnt = 0
        self.waited = {}
        self.last = None

    def wait(self, ev):
        sem, val = ev
        if self.waited.get(sem.id, 0) >= val:
            return
        self.waited[sem.id] = val
        self.h.wait_ge(sem.h, val)

    def tick(self, inst):
        if self.sem is None or self.cnt >= 30000:
            self.sem = self.K.alloc_sem(self.name)
            self.cnt = 0
        self.cnt += 1
        inst.then_inc(self.sem.h, 1)
        self.last = (self.sem, self.cnt)
        return self.last


class Tracker:
    def __init__(self, nc, es, dry):
        self.nc = nc
        self.es = es
        self.dry = dry
        self.nsem = 0
        self.pe = Eng(self, "pe", nc.tensor)
        self.act = Eng(self, "act", nc.scalar)
        self.dve = Eng(self, "dve", nc.vector)
        self.pool = Eng(self, "pool", nc.gpsimd)
        self.sp = Eng(self, "sp", nc.sync)
        self.sp_dma = {}
        self.grp = {}
        self.n_inst = 0

    def alloc_sem(self, name):
        self.nsem += 1
        h = self.es.enter_context(self.nc.semaphore(f"s_{name}_{self.nsem}"))
        return Sem(h, self.nsem)

    def _deps(self, eng, reads, writes):
        for b in reads:
            if b.w is not None:
                eng.wait(b.w)
        for b in writes:
            if b.w is not None:
                eng.wait(b.w)
            for ev in b.r.values():
                eng.wait(ev)

    def op(self, eng, fn, reads=(), writes=()):
        if self.dry:
            return
        self._deps(eng, reads, writes)
        inst = fn()
        ev = eng.tick(inst)
        self.n_inst += 1
        for b in reads:
            b.r[ev[0].id] = ev
        for b in writes:
            b.w = ev
            b.r = {}

    def mm(self, out, lhsT, rhs, start, stop, reads, pbuf, tick=False):
        if self.dry:
            return
        pe = self.pe
        for b in reads:
            if b.w is not None and b.w[0] is not pe.sem:
                pe.wait(b.w)
        if start:
            if pbuf.w is not None and pbuf.w[0] is not pe.sem:
                pe.wait(pbuf.w)
            for ev in pbuf.r.values():
                pe.wait(ev)
            self.grp[id(pbuf)] = set()
        g = self.grp[id(pbuf)]
        for b in reads:
            g.add(b)
        inst = self.nc.tensor.matmul(out, lhsT=lhsT, rhs=rhs, start=start, stop=stop)
        self.n_inst += 1
        if stop or tick:
            ev = pe.tick(inst)
            for b in g:
                b.r[ev[0].id] = ev
            if stop:
                pbuf.w = ev
                pbuf.r = {}
            else:
                g.clear()
                g.update(())

    def dma(self, eng, out, in_, reads=(), writes=(), track=True, sem_pool=None, **kw):
        if self.dry:
            return
        self._deps(eng, reads, writes)
        if sem_pool is not None:
            ent = sem_pool[0][sem_pool[1] % len(sem_pool[0])]
            sem_pool[1] += 1
            if ent[0] is None:
                ent[0] = self.alloc_sem("dp")
            if ent[1] > 0:
                eng.wait((ent[0], ent[1]))
            ent[1] += 16
            ev = (ent[0], ent[1])
        elif writes:
            b = writes[0]
            if b.sem_in is None:
                b.sem_in = self.alloc_sem("di")
            b.cnt_in += 16
            ev = (b.sem_in, b.cnt_in)
        else:
            b = reads[0]
            if b.sem_out is None:
                b.sem_out = self.alloc_sem("do")
            b.cnt_out += 16
            ev = (b.sem_out, b.cnt_out)
        eng.h.dma_start(out=out, in_=in_, **kw).then_inc(ev[0].h, 16)
        self.n_inst += 1
        for x in reads:
            x.r[ev[0].id] = ev
        for x in writes:
            x.w = ev
            x.r = {}
        if track:
            self.sp_dma[ev[0].id] = ev

    def barrier(self, hard=False):
        if self.dry:
            return
        evs = [e.last for e in (self.pe, self.act, self.dve, self.pool) if e.last is not None]
        evs += list(self.sp_dma.values())
        for e in (self.act, self.dve, self.sp) + ((self.pe,) if hard else ()):
            for ev in evs:
                e.wait(ev)
        self.sp_dma = {}


def build(dry, plan, debug_stage=None):
    nc = bass.Bass("TRN2", target_bir_lowering=False)
    es = ExitStack()
    with es:
        K = Tracker(nc, es, dry)
        act, dve, pool, sp = K.act, K.dve, K.pool, K.sp

        def din(name, shape, dt=F32):
            return nc.dram_tensor(name, list(shape), dt, kind="ExternalInput").ap()

        xo = din("xo", [D, OWN])
        xc = din("xc", [D, OWN])
        memT = din("memT", [D, 256])
        Wd = {n: din(n, WSHAPES[n]) for n in WNAMES}
        gains = din("gains", [5, D])
        subln = din("subln", [256])
        hgnorm = din("hgnorm", [1024])
        hglb = din("hglb", [2, 1024])
        lamv = din("lamv", [4, 128])
        c_R = din("c_R", [128, 512])
        c_tri = din("c_tri", [128, 128])
        c_hgm = din("c_hgm", [128, 128])
        c_id = din("c_id", [128, 128])
        c_atab = din("c_atab", [128, 128])
        c_mask = din("c_mask", [128, 1])
        c_ebt = din("c_ebt", [128, 144])
        outT = nc.dram_tensor("outT", [D, OWN], F32, kind="ExternalOutput").ap()
        KS = nc.dram_tensor("ks_scr", [2 * NT, 128, 8, T], BF16, kind="Internal").ap()
        VS = nc.dram_tensor("vs_scr", [2 * NT, 128, 4, 4, 256], BF16, kind="Internal").ap()
        KSb = [Buf(f"ks{j}") for j in range(2 * NT)]
        VSb = [Buf(f"vs{j}") for j in range(2 * NT)]

        def sb(name, shape, dt):
            return es.enter_context(nc.sbuf_tensor(name, list(shape), dt))

        uid = [0]

        def sbt(name, shape, dt):
            uid[0] += 1
            return nc.sbuf_tensor(f"{name}_{uid[0]}", list(shape), dt)

        Wt = [sb(f"wslot{s}", [128, SLOT_ELEMS], BF16) for s in range(NSLOT)]
        Wb = [[Buf(f"wslot{s}lo"), Buf(f"wslot{s}hi")] for s in range(NSLOT)]
        H = sb("H", [128, NCH, T], F32)
        Hb = [Buf(f"H{c}") for c in range(NCH)]
        XN = sb("XN", [128, NCH, T], BF16)
        XNb = [Buf(f"XN{c}") for c in range(NCH)]
        RSTD = sb("RSTD", [128, T], F32)
        RSTDb = Buf("RSTD")
        SQ = [sb(f"SQ{i}", [128, T], BF16) for i in range(2)]
        SQb = [Buf(f"SQ{i}") for i in range(2)]
        SG = [sb(f"SG{i}", [128, T], F32) for i in range(2)]
        SGb = [Buf(f"SG{i}") for i in range(2)]
        ONES = sb("ONES", [128, 128], BF16)
        IDENT = sb("IDENT", [128, 128], BF16)
        CONb = Buf("consts")
        Rm = sb("Rm", [128, 512], F32)
        TRI = sb("TRI", [128, 128], F32)
        HGM = sb("HGM", [128, 128], F32)
        ATAB = sb("ATAB", [128, 128], F32)
        CB = sb("CB", [128, 128], F32)
        CM = sb("CM", [128, 1], F32)
        EBT = sb("EBT", [128, 144], F32)
        EBC = sb("EBC", [128, 144], F32)
        SCM = sb("SCM", [128, 512], BF16)
        G = sb("G", [128, 5, NCH], F32)
        SUBG = sb("SUBG", [128, 2], F32)
        HGN = sb("HGN", [128, 8], F32)
        LBL = sb("LBL", [128, 2, 8], F32)
        LB = sb("LB", [128, 8], F32)
        OML = sb("OML", [128, 8], F32)
        LQ = sb("LQ", [128, 4, 128], F32)
        LT = sb("LT", [128, 2, 128], F32)
        LS = sb("LS", [128, 2], F32)
        NLAM = sb("NLAM", [128, 1], F32)
        KX = sb("KX", [128, 8, 256], BF16)
        VX = sb("VX", [128, 2, 1024], BF16)
        KXb = Buf("KX")
        VXb = Buf("VX")
        ST = sb("ST", [128, 8, 128], F32)
        STb = [Buf(f"ST{h}") for h in range(8)]
        STbf = sb("STbf", [128, 8, 2, 128], BF16)
        STbfb = [[Buf(f"STbf{h}_{p}") for p in range(2)] for h in range(8)]
        PS = es.enter_context(nc.psum_tensor("PS", [128, 8, 512], F32))
        PB = [Buf(f"bank{i}") for i in range(8)]
        rr = {}

        def bank(setname, banks):
            i = rr.get(setname, 0)
            rr[setname] = i + 1
            b = banks[i % len(banks)]
            return PS[:, b, :], PB[b]

        GB = [0, 1, 2, 3]

        def gbank():
            return bank("g", GB)

        wstate = {"pos": 0, "issued": 0, "released": 0}
        HALF = SLOT_ELEMS // 2

        def w_src(name, k0, nk, c0, ncol):
            return Wd[name][k0 * 128:(k0 + nk) * 128, c0:c0 + ncol].rearrange("(k p) c -> p k c", p=128)

        w_uid, w_first = [], []
        if not dry:
            seen = {}
            for d_ in plan:
                w_first.append(d_ not in seen)
                seen.setdefault(d_, len(seen))
                w_uid.append(seen[d_])
            WSC = nc.dram_tensor("w_scr", [max(1, len(seen)), 128, SLOT_ELEMS], BF16, kind="Internal").ap()
            WSCb = [Buf(f"wsc{u}") for u in range(len(seen))]
        wb_pool = [[[None, 0] for _ in range(8)], 0]

        def w_issue(i):
            d = plan[i]
            s = i % NSLOT
            u = w_uid[i]
            used = d[3] * d[5] if d[0] == "one" else HALF + d[5] * d[7]
            if not w_first[i]:
                K.dma(pool, Wt[s][:, 0:used], WSC[u][:, 0:used], reads=[WSCb[u]], writes=Wb[s], track=False)
                return
            w_issue_cast(i)
            K.dma(sp, WSC[u][:, 0:used], Wt[s][:, 0:used], reads=Wb[s], writes=[WSCb[u]], track=False, sem_pool=wb_pool)

        def w_issue_cast(i):
            d = plan[i]
            s = i % NSLOT
            if d[0] == "one":
                _, name, k0, nk, c0, ncol = d
                dst = Wt[s][:, 0:nk * ncol].rearrange("p (k c) -> p k c", c=ncol)
                K.dma(pool, dst, w_src(name, k0, nk, c0, ncol), writes=Wb[s], track=False)
            else:
                _, na, nka, c0a, nb_, nkb, c0b, ncol = d
                for hi, (nm, nk, c0) in enumerate(((na, nka, c0a), (nb_, nkb, c0b))):
                    dst = Wt[s][:, hi * HALF:hi * HALF + nk * ncol].rearrange("p (k c) -> p k c", c=ncol)
                    K.dma(pool, dst, w_src(nm, 0, nk, c0, ncol), writes=[Wb[s][hi]], track=False)

        def w_try_issue():
            while wstate["issued"] < len(plan) and wstate["issued"] - NSLOT < wstate["released"]:
                w_issue(wstate["issued"])
                wstate["issued"] += 1

        def w_req(desc):
            i = wstate["pos"]
            wstate["pos"] += 1
            if dry:
                plan.append(desc)
            else:
                assert plan[i] == desc, (i, plan[i], desc)
                w_try_issue()
                assert wstate["issued"] > i, "too many live weight pieces"
            return i % NSLOT

        def wrelease():
            if dry:
                return
            wstate["released"] += 1
            w_try_issue()

        def wpiece(name, k0, nk, c0, ncol):
            assert nk * ncol <= SLOT_ELEMS
            s = w_req(("one", name, k0, nk, c0, ncol))
            return Wt[s][:, 0:nk * ncol].rearrange("p (k c) -> p k c", c=ncol), Wb[s]

        def wpair(na, nka, c0a, nb_, nkb, c0b, ncol):
            assert nka * ncol <= HALF and nkb * ncol <= HALF
            s = w_req(("pair", na, nka, c0a, nb_, nkb, c0b, ncol))
            va = Wt[s][:, 0:nka * ncol].rearrange("p (k c) -> p k c", c=ncol)
            vb = Wt[s][:, HALF:HALF + nkb * ncol].rearrange("p (k c) -> p k c", c=ncol)
            return va, [Wb[s][0]], vb, [Wb[s][1]]

        ecnt = [0]

        def evac_copy(out, in_, reads, writes, scale=None):
            ecnt[0] += 1
            if scale is not None:
                K.op(act, lambda: nc.scalar.activation(out=out, in_=in_, func=AF.Copy, scale=scale), reads, writes)
            elif ecnt[0] % 2 == 0:
                K.op(act, lambda: nc.scalar.copy(out=out, in_=in_), reads, writes)
            else:
                K.op(dve, lambda: nc.vector.tensor_copy(out=out, in_=in_), reads, writes)

        def gemm_fm(wname, nk, c0, ncols, rhs_fn, N, evac, cb=512):
            cb = min(cb, ncols)
            for cblk in range(c0, c0 + ncols, cb):
                wv, wb = wpiece(wname, 0, nk, cblk, cb)
                for mi in range(cb // 128):
                    pap, pb = gbank()
                    for k in range(nk):
                        rap, rbufs = rhs_fn(k)
                        K.mm(pap[:, :N], lhsT=wv[:, k, mi * 128:(mi + 1) * 128], rhs=rap,
                             start=(k == 0), stop=(k == nk - 1), reads=wb + rbufs, pbuf=pb)
                    evac((cblk - c0) // 128 + mi, pap[:, :N], pb)
                wrelease()

        def gemm_tm(wname, nk, c0, ncols, lhs_fn, nsub, evac):
            for cblk in range(c0, c0 + ncols, 512):
                wv, wb = wpiece(wname, 0, nk, cblk, 512)
                for s in range(nsub):
                    pap, pb = gbank()
                    for k in range(nk):
                        lap, lbufs = lhs_fn(k, s)
                        K.mm(pap, lhsT=lap, rhs=wv[:, k, :], start=(k == 0), stop=(k == nk - 1),
                             reads=wb + lbufs, pbuf=pb)
                    evac(s, cblk - c0, pap, pb)
                wrelease()

        def rms_rstd(src_fn, nch, N, inv_n):
            pap, pb = bank("st", [7])
            for c in range(nch):
                sap, sbufs = src_fn(c)
                q = c % 2
                K.op(act, lambda: nc.scalar.activation(out=SQ[q][:, :N], in_=sap, func=AF.Square), sbufs, [SQb[q]])
                K.mm(pap[:, :N], lhsT=ONES[:], rhs=SQ[q][:, :N], start=(c == 0), stop=(c == nch - 1),
                     reads=[SQb[q], CONb], pbuf=pb)
            K.op(act, lambda: nc.scalar.activation(out=RSTD[:, :N], in_=pap[:, :N], func=AF.Sqrt, scale=inv_n, bias=EPSC[:, 0:1]),
                 [pb, CONb], [RSTDb])
            K.op(dve, lambda: nc.vector.reciprocal(out=RSTD[:, :N], in_=RSTD[:, :N]), [RSTDb], [RSTDb])

        EPSC = sb("EPSC", [128, 2], F32)
        K.op(dve, lambda: nc.vector.memset(ONES[:], 1.0), [], [CONb])
        K.op(dve, lambda: nc.vector.memset(EPSC[:, 0:1], EPS), [], [CONb])
        K.op(dve, lambda: nc.vector.memset(SCM[:], 1.0), [], [CONb])
        K.op(dve, lambda: nc.vector.memset(SCM[:].rearrange("p (c t) -> p c t", t=64)[:, :, 0:1], 0.0), [], [CONb])
        K.op(dve, lambda: nc.vector.memset(ST[:], 0.0), [], STb)
        K.op(dve, lambda: nc.vector.memset(STbf[:], 0.0), [], [b for hb in STbfb for b in hb])
        smallb = Buf("small")
        K.dma(pool, IDENT[:], c_id, writes=[Buf("ident")])
        for dst, src in ((Rm, c_R), (TRI, c_tri), (HGM, c_hgm), (ATAB, c_atab), (CM, c_mask), (EBT, c_ebt)):
            K.dma(sp, dst[:], src, writes=[smallb])
        with nc.allow_non_contiguous_dma(reason="tiny param layout loads"):
            for gi in range(5):
                K.dma(sp, G[:, gi, :], gains[gi].rearrange("(c p) -> p c", p=128), writes=[smallb])
            K.dma(sp, SUBG[:], subln.rearrange("(c p) -> p c", p=128), writes=[smallb])
            K.dma(sp, HGN[:], hgnorm.rearrange("(c p) -> p c", p=128), writes=[smallb])
            K.dma(sp, LBL[:], hglb.rearrange("r (c p) -> p r c", p=128), writes=[smallb])
        for r_ in range(4):
            K.dma(sp, LQ[:, r_, :], lamv[r_:r_ + 1, :].to_broadcast([128, 128]), writes=[smallb])
        sm = [smallb]
        K.op(dve, lambda: nc.vector.tensor_scalar(out=SUBG[:], in0=SUBG[:], scalar1=1.0 - LAMBDA_INIT, scalar2=None, op0=ALU.mult), sm, sm)
        K.op(dve, lambda: nc.vector.tensor_tensor(out=LB[:], in0=LBL[:, 0, :], in1=LBL[:, 1, :], op=ALU.subtract), sm, sm)
        K.op(act, lambda: nc.scalar.activation(out=LB[:], in_=LB[:], func=AF.Sigmoid), sm, sm)
        K.op(dve, lambda: nc.vector.tensor_scalar(out=OML[:], in0=LB[:], scalar1=-1.0, scalar2=1.0, op0=ALU.mult, op1=ALU.add), sm, sm)
        K.op(dve, lambda: nc.vector.tensor_tensor(out=LT[:, 0, :], in0=LQ[:, 0, :], in1=LQ[:, 1, :], op=ALU.mult), sm, sm)
        K.op(dve, lambda: nc.vector.tensor_tensor(out=LT[:, 1, :], in0=LQ[:, 2, :], in1=LQ[:, 3, :], op=ALU.mult), sm, sm)
        K.op(dve, lambda: nc.vector.reduce_sum(out=LS[:], in_=LT[:], axis=AX.X), sm, sm)
        K.op(act, lambda: nc.scalar.activation(out=LS[:], in_=LS[:], func=AF.Exp), sm, sm)
        K.op(dve, lambda: nc.vector.scalar_tensor_tensor(out=NLAM[:], in0=LS[:, 1:2], scalar=-LAMBDA_INIT, in1=LS[:, 0:1],
                                                          op0=ALU.add, op1=ALU.subtract), sm, sm)
        K.op(dve, lambda: nc.vector.tensor_scalar(out=CB[:], in0=ATAB[:], scalar1=CM[:, 0:1], scalar2=None, op0=ALU.add), sm, sm)
        K.op(dve, lambda: nc.vector.tensor_scalar(out=EBC[:], in0=EBT[:], scalar1=CM[:, 0:1], scalar2=None, op0=ALU.add), sm, sm)
        K.barrier(hard=True)
        CONb.w = None
        CONb.r = {}
        if not dry:
            pass

        def consts_ready():
            return []

        with ExitStack() as ph:
            MT = ph.enter_context(sbt("MT", [128, NCH, 256], F32))
            MN = ph.enter_context(sbt("MN", [128, NCH, 256], BF16))
            MTb = Buf("MT")
            MNb = [Buf(f"MN{c}") for c in range(NCH)]
            K.dma(sp, MT[:], memT.rearrange("(c p) t -> p c t", p=128), writes=[MTb])
            rms_rstd(lambda c: (MT[:, c, :], [MTb]), NCH, 256, 1.0 / D)
            for c in range(NCH):
                K.op(dve, lambda: nc.vector.scalar_tensor_tensor(out=MN[:, c, :], in0=MT[:, c, :], scalar=G[:, 4, c:c + 1],
                                                                  in1=RSTD[:, :256], op0=ALU.mult, op1=ALU.mult),
                     [MTb, RSTDb], [MNb[c]])
            gemm_fm("w_mem_kv", NCH, 0, 1024, lambda k: (MN[:, k, :], [MNb[k]]), 256,
                    lambda m, p, pb: evac_copy(KX[:, m, :], p, [pb], [KXb]))
            gemm_tm("w_mem_kv", NCH, 1024, 1024, lambda k, s: (MN[:, k, s * 128:(s + 1) * 128], [MNb[k]]), 2,
                    lambda s, co, p, pb: evac_copy(VX[:, s, co:co + 512], p, [pb], [VXb]))
            K.barrier()

        def ffn(pre, gi):
            with ExitStack() as ph:
                HID = ph.enter_context(sbt("HID", [128, NHC, T], BF16))
                HIDb = [Buf(f"HID{j}") for j in range(NHC)]
                rms_rstd(lambda c: (H[:, c, :], [Hb[c]]), NCH, T, 1.0 / D)
                for c in range(NCH):
                    K.op(dve, lambda: nc.vector.scalar_tensor_tensor(out=XN[:, c, :], in0=H[:, c, :], scalar=G[:, gi, c:c + 1],
                                                                      in1=RSTD[:], op0=ALU.mult, op1=ALU.mult),
                         [Hb[c], RSTDb], [XNb[c]])
                for jb in range(NHC // 2):
                    wg, wgb, wu, wub = wpair(pre + "_wg", NCH, jb * 256, pre + "_wu", NCH, jb * 256, 256)
                    for mi in range(2):
                        j = jb * 2 + mi
                        pg, pgb = gbank()
                        pu, pub = gbank()
                        for k in range(NCH):
                            K.mm(pg, lhsT=wg[:, k, mi * 128:(mi + 1) * 128], rhs=XN[:, k, :], start=(k == 0),
                                 stop=(k == NCH - 1), reads=wgb + [XNb[k]], pbuf=pgb)
                        for k in range(NCH):
                            K.mm(pu, lhsT=wu[:, k, mi * 128:(mi + 1) * 128], rhs=XN[:, k, :], start=(k == 0),
                                 stop=(k == NCH - 1), reads=wub + [XNb[k]], pbuf=pub)
                        q = j % 2
                        K.op(act, lambda: nc.scalar.activation(out=SG[q][:], in_=pg, func=AF.Silu), [pgb], [SGb[q]])
                        K.op(dve, lambda: nc.vector.tensor_tensor(out=HID[:, j, :], in0=SG[q][:], in1=pu, op=ALU.mult),
                             [SGb[q], pub], [HIDb[j]])
                    wrelease()
                for mb2 in range(NCH // 2):
                    pbs = [gbank(), gbank()]
                    for half in range(2):
                        wd, wdb = wpiece(pre + "_wd", half * 22, 22, mb2 * 256, 256)
                        for mi in range(2):
                            p, pb = pbs[mi]
                            for jj in range(22):
                                j = half * 22 + jj
                                K.mm(p, lhsT=wd[:, jj, mi * 128:(mi + 1) * 128], rhs=HID[:, j, :], start=(j == 0), stop=(j == NHC - 1),
                                     reads=wdb + [HIDb[j]], pbuf=pb, tick=(jj == 21))
                        wrelease()
                    for mi in range(2):
                        m = mb2 * 2 + mi
                        p, pb = pbs[mi]
                        K.op(dve, lambda: nc.vector.scalar_tensor_tensor(out=H[:, m, :], in0=p, scalar=0.5, in1=H[:, m, :],
                                                                          op0=ALU.mult, op1=ALU.add), [pb, Hb[m]], [Hb[m]])
                K.barrier()

        def load_x(src, t0):
            K.dma(sp, H[:], src.rearrange("(c p) t -> p c t", p=128)[:, :, t0:t0 + T], writes=Hb)

        def mix_norm():
            rms_rstd(lambda c: (H[:, c, :], [Hb[c]]), NCH, T, 1.0 / D)
            for c in range(NCH):
                K.op(dve, lambda: nc.vector.scalar_tensor_tensor(out=XN[:, c, :], in0=H[:, c, :], scalar=G[:, 1, c:c + 1],
                                                                  in1=RSTD[:], op0=ALU.mult, op1=ALU.mult),
                     [Hb[c], RSTDb], [XNb[c]])

        xn_rhs = lambda k: (XN[:, k, :], [XNb[k]])
        xn_lhs = lambda k, s: (XN[:, k, s * 128:(s + 1) * 128], [XNb[k]])

        def da_kv(jt):
            with ExitStack() as ph:
                KTs = ph.enter_context(sbt("KTs", [128, 8, T], BF16))
                VTs = ph.enter_context(sbt("VTs", [128, 4, 4, 256], BF16))
                KTb = [Buf(f"KTs{m}") for m in range(8)]
                VTb = [Buf(f"VTs{s}") for s in range(8)]
                gemm_fm("w_in", NCH, 1024, 1024, xn_rhs, T,
                        lambda m, p, pb: evac_copy(KTs[:, m, :], p, [pb], [KTb[m]]))
                gemm_tm("w_in", NCH, 2048, 1024, xn_lhs, 4,
                        lambda s, co, p, pb: evac_copy(VTs[:, co // 256:co // 256 + 2, s, :], p.rearrange("p (h c) -> p h c", c=256), [pb], [VTb[s * 2 + co // 512]]))
                K.dma(sp, KS[jt], KTs[:], reads=KTb, writes=[KSb[jt]])
                K.dma(sp, VS[jt], VTs[:], reads=VTb, writes=[VSb[jt]])
                K.barrier()

        def hg(own, Y):
            for Gp in range(2):
                with ExitStack() as ph:
                    def t_(name, shape, dt, st=ph):
                        return st.enter_context(sbt(name, shape, dt))
                    IH = t_("hgIH", [128, 4, 512], BF16)
                    KH = t_("hgKH", [128, 4, T], BF16)
                    Et = t_("hgEt", [128, 4, 8], F32)
                    IHb, KHb, Etb = [Buf(f"IH{s}") for s in range(4)], Buf("KH"), Buf("Et")
                    if own:
                        KTb_ = t_("hgKTb", [128, 4, T], BF16)
                        QTb_ = t_("hgQTb", [128, 4, T], BF16)
                        KTbb, QTbb = Buf("KTb"), Buf("QTb")
                    with ExitStack() as ph1:
                        A1 = t_("hgA1", [128, 4, T], F32, ph1)
                        A2 = t_("hgA2", [128, 4, T], F32, ph1)
                        A3 = t_("hgA3", [128, 4, T], F32, ph1)
                        A1b = [Buf(f"A1_{h}") for h in range(4)]
                        A2b, A3b = Buf("A2"), Buf("A3")
                        if own:
                            QH = t_("hgQH", [128, 4, T], F32, ph1)
                            QHb = [Buf(f"QH{h}") for h in range(4)]
                            gemm_fm("w_in", NCH, 3072 + Gp * 512, 512, xn_rhs, T,
                                    lambda m, p, pb: K.op(act, lambda: nc.scalar.activation(out=QH[:, m, :], in_=p, func=AF.Silu), [pb], [QHb[m]]))
                        gemm_fm("w_in", NCH, 4096 + Gp * 512, 512, xn_rhs, T,
                                lambda m, p, pb: K.op(act, lambda: nc.scalar.activation(out=A1[:, m, :], in_=p, func=AF.Sigmoid), [pb], [A1b[m]]))
                        gemm_tm("w_in", NCH, 5120 + Gp * 512, 512, xn_lhs, 4,
                                lambda s, co, p, pb: evac_copy(IH[:, s, :], p, [pb], [IHb[s]]))
                        for hh in range(4):
                            h = Gp * 4 + hh
                            K.op(dve, lambda: nc.vector.tensor_scalar(out=A1[:, hh, :], in0=A1[:, hh, :], scalar1=OML[:, h:h + 1],
                                                                      scalar2=LB[:, h:h + 1], op0=ALU.mult, op1=ALU.add), [A1b[hh]], [A1b[hh]])
                        K.op(act, lambda: nc.scalar.activation(out=A2[:], in_=A1[:], func=AF.Ln), A1b, [A2b])
                        K.op(dve, lambda: nc.vector.tensor_scalar(out=A1[:], in0=A1[:], scalar1=-1.0, scalar2=1.0, op0=ALU.mult, op1=ALU.add), A1b, A1b)
                        for hh in range(4):
                            K.op(dve, lambda: nc.vector.tensor_tensor_scan(out=A3[:, hh, :], data0=SCM[:], data1=A2[:, hh, :], initial=0.0,
                                                                           op0=ALU.mult, op1=ALU.add), [A2b], [A3b])
                        A3v = A3[:].rearrange("p h (c t) -> p h c t", t=64)
                        K.op(act, lambda: nc.scalar.activation(out=Et[:], in_=A3v[:, :, :, 63], func=AF.Exp), [A3b], [Etb])
                        K.op(act, lambda: nc.scalar.activation(out=A2[:], in_=A3[:], func=AF.Exp, scale=-1.0), [A3b, A2b], [A2b])
                        K.op(dve, lambda: nc.vector.tensor_tensor(out=A2[:], in0=A2[:], in1=A1[:], op=ALU.mult), [A2b] + A1b, [A2b])
                        K.op(dve, lambda: nc.vector.tensor_tensor(out=KH[:].rearrange("p h (c t) -> p (h c) t", t=64),
                                                                  in0=A2[:].rearrange("p h (c t) -> p (h c) t", t=64),
                                                                  in1=Et[:].rearrange("p h c -> p (h c)").unsqueeze(2).to_broadcast([128, 32, 64]),
                                                                  op=ALU.mult), [A2b, Etb], [KHb])
                        if own:
                            K.op(act, lambda: nc.scalar.copy(out=KTb_[:], in_=A2[:]), [A2b], [KTbb])
                            K.op(act, lambda: nc.scalar.activation(out=A3[:], in_=A3[:], func=AF.Exp), [A3b], [A3b])
                            K.op(dve, lambda: nc.vector.tensor_tensor(out=QTb_[:], in0=QH[:], in1=A3[:], op=ALU.mult), QHb + [A3b], [QTbb])
                        K.barrier()
                    with ExitStack() as ph2:
                        KHt = t_("hgKHt", [128, 4, 4, 128], BF16, ph2)
                        KHtb = [Buf(f"KHt{h}") for h in range(4)]
                        if own:
                            GH = t_("hgGH", [128, 4, T], BF16, ph2)
                            AT = [t_(f"hgAT{i_}", [128, 128], BF16, ph2) for i_ in range(4)]
                            OH = t_("hgOH", [128, 4, T], F32, ph2)
                            GHb = [Buf(f"GH{h}") for h in range(4)]
                            ATb = [Buf(f"AT{i_}") for i_ in range(4)]
                            OHb = [Buf(f"OH{h}") for h in range(4)]
                            gemm_fm("w_in", NCH, 6144 + Gp * 512, 512, xn_rhs, T,
                                    lambda m, p, pb: K.op(act, lambda: nc.scalar.activation(out=GH[:, m, :], in_=p, func=AF.Sigmoid), [pb], [GHb[m]]))
                        for hh in range(4):
                            pap, pb = gbank()
                            for s in range(4):
                                K.mm(pap[:, s * 128:(s + 1) * 128], lhsT=KH[:, hh, s * 128:(s + 1) * 128], rhs=IDENT[:],
                                     start=True, stop=True, reads=[KHb], pbuf=pb)
                            evac_copy(KHt[:, hh, :, :], pap.rearrange("p (s k) -> p s k", k=128), [pb], [KHtb[hh]])
                        OB = [0, 1, 2, 3]
                        for s in range(4):
                            if own:
                                for hh in range(4):
                                    sp_, spb = bank("hs", [4, 5])
                                    K.mm(sp_[:, 0:128], lhsT=KTb_[:, hh, s * 128:(s + 1) * 128], rhs=QTb_[:, hh, s * 128:(s + 1) * 128],
                                         start=True, stop=True, reads=[KTbb, QTbb], pbuf=spb)
                                    K.op(dve, lambda: nc.vector.tensor_tensor(out=AT[hh][:], in0=sp_[:, 0:128], in1=HGM[:], op=ALU.mult), [spb], [ATb[hh]])
                                for hh in range(4):
                                    K.mm(PS[:, OB[hh], 0:128], lhsT=IH[:, s, hh * 128:(hh + 1) * 128], rhs=AT[hh][:], start=True, stop=False,
                                         reads=[IHb[s], ATb[hh]], pbuf=PB[OB[hh]])
                            for c2 in range(2):
                                ch = s * 2 + c2
                                par = ch % 2
                                if own:
                                    for hh in range(4):
                                        h = Gp * 4 + hh
                                        K.mm(PS[:, OB[hh], c2 * 64:(c2 + 1) * 64], lhsT=STbf[:, h, par, :],
                                             rhs=QTb_[:, hh, s * 128 + c2 * 64:s * 128 + (c2 + 1) * 64], start=False, stop=(c2 == 1),
                                             reads=[STbfb[h][par], QTbb], pbuf=PB[OB[hh]], tick=True)
                                for hh in range(4):
                                    h = Gp * 4 + hh
                                    pp, ppb = bank("hp", [6, 7] if own else [4, 5, 6, 7])
                                    K.mm(pp[:, 0:128], lhsT=KHt[c2 * 64:(c2 + 1) * 64, hh, s, :], rhs=IH[c2 * 64:(c2 + 1) * 64, s, hh * 128:(hh + 1) * 128],
                                         start=True, stop=True, reads=[KHtb[hh], IHb[s]], pbuf=ppb)
                                    K.op(dve, lambda: nc.vector.scalar_tensor_tensor(out=ST[:, h, :], in0=ST[:, h, :], scalar=Et[:, hh, ch:ch + 1],
                                                                                      in1=pp[:, 0:128], op0=ALU.mult, op1=ALU.add),
                                         [STb[h], Etb, ppb], [STb[h]])
                                    K.op(act, lambda: nc.scalar.copy(out=STbf[:, h, 1 - par, :], in_=ST[:, h, :]), [STb[h]], [STbfb[h][1 - par]])
                            if own:
                                for hh in range(4):
                                    evac_copy(OH[:, hh, s * 128:(s + 1) * 128], PS[:, OB[hh], 0:128], [PB[OB[hh]]], [OHb[hh]])
                        if own:
                            SQ4 = KTb_
                            K.op(act, lambda: nc.scalar.activation(out=SQ4[:], in_=OH[:], func=AF.Square), OHb + [KTbb], [KTbb])
                            for hh in range(4):
                                h = Gp * 4 + hh
                                pap, pb = gbank()
                                K.mm(pap, lhsT=ONES[:], rhs=SQ4[:, hh, :], start=True, stop=True, reads=[KTbb], pbuf=pb)
                                q = hh % 2
                                K.op(act, lambda: nc.scalar.activation(out=SG[q][:], in_=pap, func=AF.Sqrt, scale=1.0 / 128, bias=EPSC[:, 0:1]), [pb], [SGb[q]])
                                K.op(dve, lambda: nc.vector.reciprocal(out=SG[q][:], in_=SG[q][:]), [SGb[q]], [SGb[q]])
                                K.op(dve, lambda: nc.vector.tensor_tensor(out=OH[:, hh, :], in0=OH[:, hh, :], in1=SG[q][:], op=ALU.mult), [OHb[hh], SGb[q]], [OHb[hh]])
                                K.op(dve, lambda: nc.vector.scalar_tensor_tensor(out=Y[0][:, 8 + h, :], in0=OH[:, hh, :], scalar=HGN[:, h:h + 1], in1=GH[:, hh, :],
                                                                                  op0=ALU.mult, op1=ALU.mult), [OHb[hh], GHb[hh]], [Y[1][8 + h]])
                        K.barrier()

        def da_attn(i, Y):
            nkt = NT + i + 1
            with ExitStack() as ph:
                def t_(name, shape, dt):
                    return ph.enter_context(sbt(name, shape, dt))
                QT = t_("daQT", [128, 8, T], BF16)
                QTb = [Buf(f"daQT{m}") for m in range(8)]
                gemm_fm("w_in", NCH, 0, 1024, xn_rhs, T,
                        lambda m, p, pb: evac_copy(QT[:, m, :], p, [pb], [QTb[m]], scale=128 ** -0.5))
                VBh = t_("daVB", [128, 2 * NT, 4, 256], BF16)
                KB = t_("daKB0", [128, 2 * NT, T], BF16)
                TT = [t_(f"daTT{i_}", [128, T], F32) for i_ in range(2)] + [SG[0], SG[1]]
                PT = [t_(f"daPT{i_}", [128, T], BF16) for i_ in range(4)]
                RD = RSTD
                O0 = t_("daO0", [128, 2, T], F32)
                YD = t_("daYD", [128, 2, T], F32)
                VBb, KBb = Buf("VBh"), Buf("KB0")
                TTb = [Buf("TT0"), Buf("TT1"), SGb[0], SGb[1]]
                PTb = [Buf(f"PT{i_}") for i_ in range(4)]
                RDb, O0b, YDb = RSTDb, [Buf("O0_0"), Buf("O0_1")], [Buf("YD0"), Buf("YD1")]
                all_blocks = [(j, kb) for j in range(nkt) for kb in range(4)]

                def bdist(j, kb):
                    return (j - (NT + i)) * 512 + kb * 128

                MIN_DIST = {0: -512, 1: -1792}
                DEPTH = 3
                SB = [0, 1, 2, 3, 4]
                accs = [5, 6, 7]
                for h in range(4):
                    K.dma(sp, VBh[:, 0:nkt, :, :].rearrange("p j s c -> p j (s c)"), VS[0:nkt, :, h, :, :].rearrange("j p s c -> p j (s c)"),
                          reads=VSb[:nkt], writes=[VBb])
                    blocks = [(j, kb) for (j, kb) in all_blocks if bdist(j, kb) >= MIN_DIST.get(h, -10 ** 9)]
                    nb = len(blocks)
                    for c in range(2):
                        hc = h * 2 + c
                        K.dma(sp, KB[:, 0:nkt, :], KS[0:nkt, :, hc, :].rearrange("j p t -> p j t"), reads=KSb[:nkt], writes=[KBb])

                        def geom(bi):
                            j, kb = blocks[bi]
                            diag = (j == nkt - 1)
                            qs = kb * 128 if diag else 0
                            return j, kb, diag, qs, T - qs

                        def s_stage(bi):
                            j, kb, diag, qs, n = geom(bi)
                            sp_, spb = bank("das", SB)
                            K.mm(sp_[:, :n], lhsT=KB[:, j, kb * 128:(kb + 1) * 128], rhs=QT[:, hc, qs:], start=True, stop=True,
                                 reads=[KBb, QTb[hc]], pbuf=spb)
                            return sp_, spb

                        sq_ = {}
                        for bi in range(min(DEPTH, nb)):
                            sq_[bi] = s_stage(bi)
                        for bi in range(nb):
                            if bi + DEPTH < nb:
                                sq_[bi + DEPTH] = s_stage(bi + DEPTH)
                            sp_, spb = sq_.pop(bi)
                            j, kb, diag, qs, n = geom(bi)
                            dist = bdist(j, kb)
                            q = bi % 4
                            if h == 0:
                                K.op(dve, lambda: nc.vector.scalar_tensor_tensor(out=TT[q][:, :n], in0=Rm[:, :n], scalar=SLOPES[h], in1=sp_[:, :n],
                                                                                  op0=ALU.mult, op1=ALU.add), [spb], [TTb[q]])
                                if diag:
                                    K.op(dve, lambda: nc.vector.tensor_tensor(out=TT[q][:, 0:128], in0=TT[q][:, 0:128], in1=TRI[:], op=ALU.add), [TTb[q]], [TTb[q]])
                                    bias = 0.0
                                elif j < NT:
                                    nidx = (-dist) // 128
                                    bias = CB[:, h * 32 + nidx:h * 32 + nidx + 1]
                                else:
                                    bias = float(SLOPES[h] * dist)
                                K.op(act, lambda: nc.scalar.activation(out=PT[q][:, :n], in_=TT[q][:, :n], func=AF.Exp, bias=bias), [TTb[q]], [PTb[q]])
                            else:
                                col = h * 36 + 32 + dist // 128
                                bcol = (EBC if j < NT else EBT)[:, col:col + 1]
                                if diag:
                                    K.op(dve, lambda: nc.vector.tensor_tensor(out=TT[q][:, 0:128], in0=sp_[:, 0:128], in1=TRI[:], op=ALU.add), [spb], [TTb[q]])
                                    K.op(act, lambda: nc.scalar.activation(out=PT[q][:, 0:128], in_=TT[q][:, 0:128], func=AF.Exp, bias=bcol), [TTb[q]], [PTb[q]])
                                    if n > 128:
                                        K.op(act, lambda: nc.scalar.activation(out=PT[q][:, 128:n], in_=sp_[:, 128:n], func=AF.Exp, bias=bcol), [spb], [PTb[q]])
                                else:
                                    K.op(act, lambda: nc.scalar.activation(out=PT[q][:, :n], in_=sp_[:, :n], func=AF.Exp, bias=bcol), [spb], [PTb[q]])
                            last = (bi == nb - 1)
                            for e in range(2):
                                K.mm(PS[:, accs[e], qs:], lhsT=VBh[:, j, kb, e * 128:(e + 1) * 128], rhs=PT[q][:, :n], start=(bi == 0), stop=last,
                                     reads=[VBb, PTb[q]], pbuf=PB[accs[e]])
                            K.mm(PS[:, accs[2], qs:], lhsT=ONES[:], rhs=PT[q][:, :n], start=(bi == 0), stop=last, reads=[PTb[q]], pbuf=PB[accs[2]])
                        K.op(dve, lambda: nc.vector.reciprocal(out=RD[:], in_=PS[:, accs[2], :]), [PB[accs[2]]], [RDb])
                        for e in range(2):
                            if c == 0:
                                K.op(dve, lambda: nc.vector.tensor_tensor(out=O0[:, e, :], in0=PS[:, accs[e], :], in1=RD[:], op=ALU.mult),
                                     [PB[accs[e]], RDb], [O0b[e]])
                            else:
                                K.op(dve, lambda: nc.vector.tensor_tensor(out=YD[:, e, :], in0=PS[:, accs[e], :], in1=RD[:], op=ALU.mult),
                                     [PB[accs[e]], RDb], [YDb[e]])
                                K.op(dve, lambda: nc.vector.scalar_tensor_tensor(out=YD[:, e, :], in0=YD[:, e, :], scalar=NLAM[:, 0:1], in1=O0[:, e, :],
                                                                                  op0=ALU.mult, op1=ALU.add), [YDb[e], O0b[e]], [YDb[e]])
                        if c == 1:
                            pap, pb = bank("das", SB)
                            for e in range(2):
                                K.op(act, lambda: nc.scalar.activation(out=SQ[e][:], in_=YD[:, e, :], func=AF.Square), [YDb[e]], [SQb[e]])
                                K.mm(pap, lhsT=ONES[:], rhs=SQ[e][:], start=(e == 0), stop=(e == 1), reads=[SQb[e]], pbuf=pb)
                            K.op(act, lambda: nc.scalar.activation(out=RD[:], in_=pap, func=AF.Sqrt, scale=1.0 / 256, bias=EPSC[:, 0:1]), [pb], [RDb])
                            K.op(dve, lambda: nc.vector.reciprocal(out=RD[:], in_=RD[:]), [RDb], [RDb])
                            for e in range(2):
                                K.op(dve, lambda: nc.vector.scalar_tensor_tensor(out=Y[0][:, h * 2 + e, :], in0=YD[:, e, :], scalar=SUBG[:, e:e + 1], in1=RD[:],
                                                                                  op0=ALU.mult, op1=ALU.mult), [YDb[e], RDb], [Y[1][h * 2 + e]])
                K.barrier()

        def xa(Y):
            with ExitStack() as ph:
                QX = ph.enter_context(sbt("xaQX", [128, 8, T], BF16))
                PT = [ph.enter_context(sbt(f"xaPT{i_}", [128, T], BF16)) for i_ in range(2)]
                RD = ph.enter_context(sbt("xaRD", [128, T], F32))
                QXb = [Buf(f"QX{m}") for m in range(8)]
                PTb = [Buf("xPT0"), Buf("xPT1")]
                RDb = Buf("xRD")
                gemm_fm("w_in", NCH, 7168, 1024, xn_rhs, T,
                        lambda m, p, pb: evac_copy(QX[:, m, :], p, [pb], [QXb[m]], scale=256 ** -0.5))
                xblocks = [(h, mb) for h in range(4) for mb in range(2)]
                SB = [0, 1, 2, 3, 4]
                accs = [5, 6, 7]

                def xs_stage(bi):
                    h, mb = xblocks[bi]
                    sp_, spb = bank("das", SB)
                    for dc in range(2):
                        K.mm(sp_, lhsT=KX[:, h * 2 + dc, mb * 128:(mb + 1) * 128], rhs=QX[:, h * 2 + dc, :], start=(dc == 0), stop=(dc == 1),
                             reads=[KXb, QXb[h * 2 + dc]], pbuf=spb)
                    return sp_, spb

                xq = {0: xs_stage(0), 1: xs_stage(1)}
                for bi, (h, mb) in enumerate(xblocks):
                    if bi + 2 < len(xblocks):
                        xq[bi + 2] = xs_stage(bi + 2)
                    sp_, spb = xq.pop(bi)
                    K.op(act, lambda: nc.scalar.activation(out=PT[mb][:], in_=sp_, func=AF.Exp), [spb], [PTb[mb]])
                    for e in range(2):
                        K.mm(PS[:, accs[e], :], lhsT=VX[:, mb, h * 256 + e * 128:h * 256 + (e + 1) * 128], rhs=PT[mb][:], start=(mb == 0), stop=(mb == 1),
                             reads=[VXb, PTb[mb]], pbuf=PB[accs[e]])
                    K.mm(PS[:, accs[2], :], lhsT=ONES[:], rhs=PT[mb][:], start=(mb == 0), stop=(mb == 1), reads=[PTb[mb]], pbuf=PB[accs[2]])
                    if mb == 1:
                        K.op(dve, lambda: nc.vector.reciprocal(out=RD[:], in_=PS[:, accs[2], :]), [PB[accs[2]]], [RDb])
                        for e in range(2):
                            K.op(dve, lambda: nc.vector.tensor_tensor(out=Y[0][:, 16 + h * 2 + e, :], in0=PS[:, accs[e], :], in1=RD[:], op=ALU.mult),
                                 [PB[accs[e]], RDb], [Y[1][16 + h * 2 + e]])
                K.barrier()

        def merge_out(Y):
            with ExitStack() as ph:
                MACC = ph.enter_context(sbt("MACC", [128, 4, T], F32))
                MRG = ph.enter_context(sbt("MRG", [128, NCH, T], BF16))
                MACCb = [Buf(f"MACC{m}") for m in range(4)]
                MRGb = [Buf(f"MRG{m}") for m in range(NCH)]
                for mb2 in range(8):
                    for b in range(3):
                        gw, gwb, bw, bwb = wpair("w_in", NCH, 8192 + b * 2048 + mb2 * 256, f"wb{b}", 8, mb2 * 256, 256)
                        for mi in range(2):
                            m = mb2 * 2 + mi
                            pg, pgb = gbank()
                            pr, prb = gbank()
                            for k in range(NCH):
                                K.mm(pg, lhsT=gw[:, k, mi * 128:(mi + 1) * 128], rhs=XN[:, k, :], start=(k == 0), stop=(k == NCH - 1),
                                     reads=gwb + [XNb[k]], pbuf=pgb)
                            for k in range(8):
                                K.mm(pr, lhsT=bw[:, k, mi * 128:(mi + 1) * 128], rhs=Y[0][:, b * 8 + k, :], start=(k == 0), stop=(k == 7),
                                     reads=bwb + [Y[1][b * 8 + k]], pbuf=prb)
                            q = (b * 2 + mi) % 2
                            K.op(act, lambda: nc.scalar.activation(out=SG[q][:], in_=pg, func=AF.Sigmoid), [pgb], [SGb[q]])
                            if b == 0:
                                K.op(dve, lambda: nc.vector.tensor_tensor(out=MACC[:, mi, :], in0=SG[q][:], in1=pr, op=ALU.mult), [SGb[q], prb], [MACCb[mi]])
                            else:
                                K.op(dve, lambda: nc.vector.tensor_tensor(out=SG[q][:], in0=SG[q][:], in1=pr, op=ALU.mult), [SGb[q], prb], [SGb[q]])
                                if b == 1:
                                    K.op(dve, lambda: nc.vector.tensor_tensor(out=MACC[:, mi, :], in0=MACC[:, mi, :], in1=SG[q][:], op=ALU.add),
                                         [SGb[q], MACCb[mi]], [MACCb[mi]])
                                else:
                                    K.op(dve, lambda: nc.vector.tensor_tensor(out=MRG[:, m, :], in0=MACC[:, mi, :], in1=SG[q][:], op=ALU.add),
                                         [SGb[q], MACCb[mi]], [MRGb[m]])
                        wrelease()
                gemm_fm("w_out", NCH, 0, D, lambda k: (MRG[:, k, :], [MRGb[k]]), T,
                        lambda m, p, pb: K.op(dve, lambda: nc.vector.tensor_tensor(out=H[:, m, :], in0=p, in1=H[:, m, :], op=ALU.add), [pb, Hb[m]], [Hb[m]]))
                K.barrier()

        OUTC = [sb(f"OUTC{i_}", [128, T], F32) for i_ in range(2)]
        OUTCb = [Buf("OUTC0"), Buf("OUTC1")]

        def final_out(t0, gi=3, do_norm=True):
            if do_norm:
                rms_rstd(lambda c: (H[:, c, :], [Hb[c]]), NCH, T, 1.0 / D)
            ov = outT.rearrange("(c p) t -> p c t", p=128)
            for c in range(NCH):
                q = c % 2
                if do_norm:
                    K.op(dve, lambda: nc.vector.scalar_tensor_tensor(out=OUTC[q][:], in0=H[:, c, :], scalar=G[:, gi, c:c + 1], in1=RSTD[:],
                                                                      op0=ALU.mult, op1=ALU.mult), [Hb[c], RSTDb], [OUTCb[q]])
                else:
                    K.op(dve, lambda: nc.vector.tensor_copy(out=OUTC[q][:], in_=H[:, c, :]), [Hb[c]], [OUTCb[q]])
                K.dma(sp, ov[:, c, t0:t0 + T], OUTC[q][:], reads=[OUTCb[q]])
            K.barrier()

        stage = debug_stage
        if stage == "ffn1":
            for i in range(NT):
                load_x(xo, i * T)
                ffn("ffn1", 0)
                final_out(i * T, do_norm=False)
        else:
            load_x(xc, 0)
            for j in range(NT):
                ffn("ffn1", 0)
                mix_norm()
                if j + 1 < NT:
                    load_x(xc, (j + 1) * T)
                else:
                    load_x(xo, 0)
                da_kv(j)
                hg(False, None)
            for i in range(NT):
                if i > 0:
                    load_x(xo, i * T)
                ffn("ffn1", 0)
                mix_norm()
                da_kv(NT + i)
                with ExitStack() as yph:
                    Yt = yph.enter_context(sbt("Y", [128, 24, T], BF16))
                    Y = (Yt, [Buf(f"Y{m}") for m in range(24)])
                    da_attn(i, Y)
                    hg(True, Y)
                    xa(Y)
                    merge_out(Y)
                ffn("ffn2", 2)
                final_out(i * T)
        if not dry:
            for ev in list(K.sp_dma.values()):
                sp.wait(ev)
            for b in OUTCb:
                for ev in b.r.values():
                    sp.wait(ev)
        build.stats = (K.n_inst, K.nsem)
    return nc


_CONST_CACHE = {}


def _consts():
    if _CONST_CACHE:
        return _CONST_CACHE
    ki = np.arange(128, dtype=np.float32)[:, None]
    qi = np.arange(512, dtype=np.float32)[None, :]
    R = (ki - qi).astype(np.float32)
    tri = np.where(ki <= qi[:, :128], 0.0, NEG).astype(np.float32)
    s_ = np.arange(128)[:, None]
    t_ = np.arange(128)[None, :]
    hgm = ((s_ // 64 == t_ // 64) & (s_ <= t_)).astype(np.float32)
    ident = np.eye(128, dtype=np.float32)
    atab = np.zeros((128, 128), np.float32)
    for h in range(4):
        for n in range(32):
            atab[:, h * 32 + n] = -SLOPES[h] * 128.0 * n
    ebt = np.zeros((128, 144), np.float32)
    for h in range(4):
        for d in range(36):
            ebt[:, h * 36 + d] = SLOPES[h] * (np.arange(128, dtype=np.float32) + (d - 32) * 128.0)
    _CONST_CACHE.update(c_R=R, c_tri=tri, c_hgm=hgm, c_id=ident, c_atab=atab, c_ebt=ebt)
    return _CONST_CACHE


def _in_maps(inputs):
    x = np.asarray(inputs["x"], np.float32)
    mem = np.asarray(inputs["mem"], np.float32)
    w = {
        "ffn1_wg": inputs["ffn1_w_gate"][0], "ffn1_wu": inputs["ffn1_w_up"][0], "ffn1_wd": inputs["ffn1_w_down"][0],
        "w_in": inputs["w_in"][0], "w_mem_kv": inputs["w_mem_kv"][0], "wb0": inputs["w_branch_da"][0],
        "wb1": inputs["w_branch_hg"][0], "wb2": inputs["w_branch_xa"][0], "w_out": inputs["w_out"][0],
        "ffn2_wg": inputs["ffn2_w_gate"][0], "ffn2_wu": inputs["ffn2_w_up"][0], "ffn2_wd": inputs["ffn2_w_down"][0],
    }
    w = {k: np.ascontiguousarray(np.asarray(v, np.float32)) for k, v in w.items()}
    gains = np.ascontiguousarray(np.stack([np.asarray(inputs["ffn1_norm"][0]), np.asarray(inputs["mix_norm"][0]),
                                           np.asarray(inputs["ffn2_norm"][0]), np.asarray(inputs["final_norm"]),
                                           np.asarray(inputs["mem_norm"][0])]).astype(np.float32))
    lamv = np.ascontiguousarray(np.stack([np.asarray(inputs["da_lambda_q1"][0]), np.asarray(inputs["da_lambda_k1"][0]),
                                          np.asarray(inputs["da_lambda_q2"][0]), np.asarray(inputs["da_lambda_k2"][0])]).astype(np.float32))
    common = dict(w)
    common.update(gains=gains, subln=np.ascontiguousarray(np.asarray(inputs["da_subln"][0], np.float32)),
                  hgnorm=np.ascontiguousarray(np.asarray(inputs["hg_norm"][0], np.float32)),
                  hglb=np.ascontiguousarray(np.asarray(inputs["hg_lb_logits"], np.float32)), lamv=lamv)
    common.update(_consts())
    maps = []
    for c in range(8):
        b, half = c // 2, c % 2
        m = dict(common)
        m["xo"] = np.ascontiguousarray(x[b, half * OWN:(half + 1) * OWN, :].T)
        if half == 1:
            m["xc"] = np.ascontiguousarray(x[b, 0:OWN, :].T)
            m["c_mask"] = np.zeros((128, 1), np.float32)
        else:
            m["xc"] = np.zeros((D, OWN), np.float32)
            m["c_mask"] = np.full((128, 1), NEG, np.float32)
        m["memT"] = np.ascontiguousarray(mem[b].T)
        maps.append(m)
    return maps


_NC_CACHE = {}


def _get_nc(stage=None):
    if stage not in _NC_CACHE:
        plan = []
        build(True, plan, stage)
        _NC_CACHE[stage] = build(False, plan, stage)
    return _NC_CACHE[stage]


def kernel(**inputs):
    nc = _get_nc(None)
    maps = _in_maps(inputs)
    res = run_bass_kernel_spmd(nc, maps, core_ids=list(range(8)))
    out = np.empty((4, 4096, D), np.float32)
    for c in range(8):
        b, half = c // 2, c % 2
        out[b, half * OWN:(half + 1) * OWN, :] = np.asarray(res.results[c]["outT"]).T
    return out
```

```python
import numpy as np
import concourse.bass as bass
import concourse.mybir as mybir
from concourse.bass_utils import run_bass_kernel_spmd
from contextlib import ExitStack

F32 = mybir.dt.float32
BF16 = mybir.dt.bfloat16
AF = mybir.ActivationFunctionType
ALU = mybir.AluOpType
AX = mybir.AxisListType

D = 2048
DFF = 5632
NCH = 16
NHC = 44
T = 512
OWN = 2048
NT = OWN // T
EPS = 1e-6
LAMBDA_INIT = 0.2
SLOPES = [2.0 ** (-8.0 * (i + 1) / 4) for i in range(4)]
NEG = -30000.0
NSLOT = 3
USE_WCACHE = False
SLOT_ELEMS = 8192

WNAMES = ["ffn1_wg", "ffn1_wu", "ffn1_wd", "w_in", "w_mem_kv", "wb0", "wb1", "wb2", "w_out",
          "ffn2_wg", "ffn2_wu", "ffn2_wd"]
WSHAPES = {"ffn1_wg": (D, DFF), "ffn1_wu": (D, DFF), "ffn1_wd": (DFF, D), "w_in": (D, 14336),
           "w_mem_kv": (D, 2048), "wb0": (1024, D), "wb1": (1024, D), "wb2": (1024, D), "w_out": (D, D),
           "ffn2_wg": (D, DFF), "ffn2_wu": (D, DFF), "ffn2_wd": (DFF, D)}


class Sem:
    __slots__ = ("h", "id")

    def __init__(self, h, i):
        self.h = h
        self.id = i


class Buf:
    __slots__ = ("name", "w", "r", "sem_in", "sem_out", "cnt_in", "cnt_out")

    def __init__(self, name):
        self.name = name
        self.w = None
        self.r = {}
        self.sem_in = None
        self.sem_out = None
        self.cnt_in = 0
        self.cnt_out = 0


class Eng:
    def __init__(self, K, name, h):
        self.K = K
        self.name = name
        self.h = h
        self.sem = None
        self.cnt = 0
        self.waited = {}
        self.last = None

    def wait(self, ev):
        sem, val = ev
        if self.waited.get(sem.id, 0) >= val:
            return
        self.waited[sem.id] = val
        self.h.wait_ge(sem.h, val)

    def tick(self, inst):
        if self.sem is None or self.cnt >= 30000:
            self.sem = self.K.alloc_sem(self.name)
            self.cnt = 0
        self.cnt += 1
        inst.then_inc(self.sem.h, 1)
        self.last = (self.sem, self.cnt)
        return self.last


class Tracker:
    def __init__(self, nc, es, dry):
        self.nc = nc
        self.es = es
        self.dry = dry
        self.nsem = 0
        self.pe = Eng(self, "pe", nc.tensor)
        self.act = Eng(self, "act", nc.scalar)
        self.dve = Eng(self, "dve", nc.vector)
        self.pool = Eng(self, "pool", nc.gpsimd)
        self.sp = Eng(self, "sp", nc.sync)
        self.sp_dma = {}
        self.grp = {}
        self.n_inst = 0

    def alloc_sem(self, name):
        self.nsem += 1
        h = self.es.enter_context(self.nc.semaphore(f"s_{name}_{self.nsem}"))
        return Sem(h, self.nsem)

    def _deps(self, eng, reads, writes):
        for b in reads:
            if b.w is not None:
                eng.wait(b.w)
        for b in writes:
            if b.w is not None:
                eng.wait(b.w)
            for ev in b.r.values():
                eng.wait(ev)

    def op(self, eng, fn, reads=(), writes=()):
        if self.dry:
            return
        self._deps(eng, reads, writes)
        inst = fn()
        ev = eng.tick(inst)
        self.n_inst += 1
        for b in reads:
            b.r[ev[0].id] = ev
        for b in writes:
            b.w = ev
            b.r = {}

    def mm(self, out, lhsT, rhs, start, stop, reads, pbuf, tick=False):
        if self.dry:
            return
        pe = self.pe
        for b in reads:
            if b.w is not None and b.w[0] is not pe.sem:
                pe.wait(b.w)
        if start:
            if pbuf.w is not None and pbuf.w[0] is not pe.sem:
                pe.wait(pbuf.w)
            for ev in pbuf.r.values():
                pe.wait(ev)
            self.grp[id(pbuf)] = set()
        g = self.grp[id(pbuf)]
        for b in reads:
            g.add(b)
        inst = self.nc.tensor.matmul(out, lhsT=lhsT, rhs=rhs, start=start, stop=stop)
        self.n_inst += 1
        if stop or tick:
            ev = pe.tick(inst)
            for b in g:
                b.r[ev[0].id] = ev
            if stop:
                pbuf.w = ev
                pbuf.r = {}
            else:
                g.clear()
                g.update(())

    def dma(self, eng, out, in_, reads=(), writes=(), track=True, sem_pool=None, **kw):
        if self.dry:
            return
        self._deps(eng, reads, writes)
        if sem_pool is not None:
            ent = sem_pool[0][sem_pool[1] % len(sem_pool[0])]
            sem_pool[1] += 1
            if ent[0] is None:
                ent[0] = self.alloc_sem("dp")
            if ent[1] > 0:
                eng.wait((ent[0], ent[1]))
            ent[1] += 16
            ev = (ent[0], ent[1])
        elif writes:
            b = writes[0]
            if b.sem_in is None:
                b.sem_in = self.alloc_sem("di")
            b.cnt_in += 16
            ev = (b.sem_in, b.cnt_in)
        else:
            b = reads[0]
            if b.sem_out is None:
                b.sem_out = self.alloc_sem("do")
            b.cnt_out += 16
            ev = (b.sem_out, b.cnt_out)
        eng.h.dma_start(out=out, in_=in_, **kw).then_inc(ev[0].h, 16)
        self.n_inst += 1
        for x in reads:
            x.r[ev[0].id] = ev
        for x in writes:
            x.w = ev
            x.r = {}
        if track:
            self.sp_dma[ev[0].id] = ev

    def barrier(self, hard=False):
        if self.dry:
            return
        evs = [e.last for e in (self.pe, self.act, self.dve, self.pool) if e.last is not None]
        evs += list(self.sp_dma.values())
        for e in (self.act, self.dve, self.sp) + ((self.pe,) if hard else ()):
            for ev in evs:
                e.wait(ev)
        self.sp_dma = {}


def build(dry, plan, debug_stage=None):
    nc = bass.Bass("TRN2", target_bir_lowering=False)
    es = ExitStack()
    with es:
        K = Tracker(nc, es, dry)
        act, dve, pool, sp = K.act, K.dve, K.pool, K.sp

        def din(name, shape, dt=F32):
            return nc.dram_tensor(name, list(shape), dt, kind="ExternalInput").ap()

        xo = din("xo", [D, OWN])
        xc = din("xc", [D, OWN])
        memT = din("memT", [D, 256])
        Wd = {n: din(n, WSHAPES[n]) for n in WNAMES}
        gains = din("gains", [5, D])
        subln = din("subln", [256])
        hgnorm = din("hgnorm", [1024])
        hglb = din("hglb", [2, 1024])
        lamv = din("lamv", [4, 128])
        c_R = din("c_R", [128, 512])
        c_tri = din("c_tri", [128, 128])
        c_hgm = din("c_hgm", [128, 128])
        c_id = din("c_id", [128, 128])
        c_atab = din("c_atab", [128, 128])
        c_mask = din("c_mask", [128, 1])
        c_ebt = din("c_ebt", [128, 144])
        outT = nc.dram_tensor("outT", [D, OWN], F32, kind="ExternalOutput").ap()
        KS = nc.dram_tensor("ks_scr", [2 * NT, 128, 8, T], BF16, kind="Internal").ap()
        VS = nc.dram_tensor("vs_scr", [2 * NT, 128, 4, 4, 256], BF16, kind="Internal").ap()
        KSb = [Buf(f"ks{j}") for j in range(2 * NT)]
        VSb = [Buf(f"vs{j}") for j in range(2 * NT)]

        def sb(name, shape, dt):
            return es.enter_context(nc.sbuf_tensor(name, list(shape), dt))

        uid = [0]

        def sbt(name, shape, dt):
            uid[0] += 1
            return nc.sbuf_tensor(f"{name}_{uid[0]}", list(shape), dt)

        Wt = [sb(f"wslot{s}", [128, SLOT_ELEMS], BF16) for s in range(NSLOT)]
        Wb = [[Buf(f"wslot{s}lo"), Buf(f"wslot{s}hi")] for s in range(NSLOT)]
        H = sb("H", [128, NCH, T], F32)
        Hb = [Buf(f"H{c}") for c in range(NCH)]
        XN = sb("XN", [128, NCH, T], BF16)
        XNb = [Buf(f"XN{c}") for c in range(NCH)]
        RSTD = sb("RSTD", [128, T], F32)
        RSTDb = Buf("RSTD")
        SQ = [sb(f"SQ{i}", [128, T], BF16) for i in range(2)]
        SQb = [Buf(f"SQ{i}") for i in range(2)]
        SG = [sb(f"SG{i}", [128, T], F32) for i in range(2)]
        SGb = [Buf(f"SG{i}") for i in range(2)]
        ONES = sb("ONES", [128, 128], BF16)
        IDENT = sb("IDENT", [128, 128], BF16)
        CONb = Buf("consts")
        Rm = sb("Rm", [128, 512], F32)
        TRI = sb("TRI", [128, 128], F32)
        HGM = sb("HGM", [128, 128], F32)
        ATAB = sb("ATAB", [128, 128], F32)
        CB = sb("CB", [128, 128], F32)
        CM = sb("CM", [128, 1], F32)
        EBT = sb("EBT", [128, 144], F32)
        EBC = sb("EBC", [128, 144], F32)
        SCM = sb("SCM", [128, 512], BF16)
        G = sb("G", [128, 5, NCH], F32)
        SUBG = sb("SUBG", [128, 2], F32)
        HGN = sb("HGN", [128, 8], F32)
        LBL = sb("LBL", [128, 2, 8], F32)
        LB = sb("LB", [128, 8], F32)
        OML = sb("OML", [128, 8], F32)
        LQ = sb("LQ", [128, 4, 128], F32)
        LT = sb("LT", [128, 2, 128], F32)
        LS = sb("LS", [128, 2], F32)
        NLAM = sb("NLAM", [128, 1], F32)
        KX = sb("KX", [128, 8, 256], BF16)
        VX = sb("VX", [128, 2, 1024], BF16)
        KXb = Buf("KX")
        VXb = Buf("VX")
        ST = sb("ST", [128, 8, 128], F32)
        STb = [Buf(f"ST{h}") for h in range(8)]
        STbf = sb("STbf", [128, 8, 2, 128], BF16)
        STbfb = [[Buf(f"STbf{h}_{p}") for p in range(2)] for h in range(8)]
        PS = es.enter_context(nc.psum_tensor("PS", [128, 8, 512], F32))
        PB = [Buf(f"bank{i}") for i in range(8)]
        rr = {}

        def bank(setname, banks):
            i = rr.get(setname, 0)
            rr[setname] = i + 1
            b = banks[i % len(banks)]
            return PS[:, b, :], PB[b]

        GB = [0, 1, 2, 3]

        def gbank():
            return bank("g", GB)

        wstate = {"pos": 0, "issued": 0, "released": 0}
        HALF = SLOT_ELEMS // 2

        def w_src(name, k0, nk, c0, ncol):
            return Wd[name][k0 * 128:(k0 + nk) * 128, c0:c0 + ncol].rearrange("(k p) c -> p k c", p=128)

        w_uid, w_first = [], []
        if not dry:
            seen = {}
            for d_ in plan:
                w_first.append(True if not USE_WCACHE else d_ not in seen)
                seen.setdefault(d_, len(seen))
                w_uid.append(seen[d_])
            WSC = nc.dram_tensor("w_scr", [max(1, len(seen)) if USE_WCACHE else 1, 128, SLOT_ELEMS], BF16, kind="Internal").ap()
            WSCb = [Buf(f"wsc{u}") for u in range(len(seen))]
        wb_pool = [[[None, 0] for _ in range(8)], 0]

        def w_issue(i):
            d = plan[i]
            s = i % NSLOT
            u = w_uid[i]
            used = d[3] * d[5] if d[0] == "one" else HALF + d[5] * d[7]
            if not w_first[i]:
                K.dma(pool, Wt[s][:, 0:used], WSC[u][:, 0:used], reads=[WSCb[u]], writes=Wb[s], track=False)
                return
            w_issue_cast(i)
            if USE_WCACHE:
                K.dma(sp, WSC[u][:, 0:used], Wt[s][:, 0:used], reads=Wb[s], writes=[WSCb[u]], track=False, sem_pool=wb_pool)

        def w_issue_cast(i):
            d = plan[i]
            s = i % NSLOT
            if d[0] == "one":
                _, name, k0, nk, c0, ncol = d
                dst = Wt[s][:, 0:nk * ncol].rearrange("p (k c) -> p k c", c=ncol)
                K.dma(pool, dst, w_src(name, k0, nk, c0, ncol), writes=Wb[s], track=False)
            else:
                _, na, nka, c0a, nb_, nkb, c0b, ncol = d
                for hi, (nm, nk, c0) in enumerate(((na, nka, c0a), (nb_, nkb, c0b))):
                    dst = Wt[s][:, hi * HALF:hi * HALF + nk * ncol].rearrange("p (k c) -> p k c", c=ncol)
                    K.dma(pool, dst, w_src(nm, 0, nk, c0, ncol), writes=[Wb[s][hi]], track=False)

        def w_try_issue():
            while wstate["issued"] < len(plan) and wstate["issued"] - NSLOT < wstate["released"]:
                w_issue(wstate["issued"])
                wstate["issued"] += 1

        def w_req(desc):
            i = wstate["pos"]
            wstate["pos"] += 1
            if dry:
                plan.append(desc)
            else:
                assert plan[i] == desc, (i, plan[i], desc)
                w_try_issue()
                assert wstate["issued"] > i, "too many live weight pieces"
            return i % NSLOT

        def wrelease():
            if dry:
                return
            wstate["released"] += 1
            w_try_issue()

        def wpiece(name, k0, nk, c0, ncol):
            assert nk * ncol <= SLOT_ELEMS
            s = w_req(("one", name, k0, nk, c0, ncol))
            return Wt[s][:, 0:nk * ncol].rearrange("p (k c) -> p k c", c=ncol), Wb[s]

        def wpair(na, nka, c0a, nb_, nkb, c0b, ncol):
            assert nka * ncol <= HALF and nkb * ncol <= HALF
            s = w_req(("pair", na, nka, c0a, nb_, nkb, c0b, ncol))
            va = Wt[s][:, 0:nka * ncol].rearrange("p (k c) -> p k c", c=ncol)
            vb = Wt[s][:, HALF:HALF + nkb * ncol].rearrange("p (k c) -> p k c", c=ncol)
            return va, [Wb[s][0]], vb, [Wb[s][1]]

        ecnt = [0]

        def evac_copy(out, in_, reads, writes, scale=None):
            ecnt[0] += 1
            if scale is not None:
                K.op(act, lambda: nc.scalar.activation(out=out, in_=in_, func=AF.Copy, scale=scale), reads, writes)
            elif ecnt[0] % 2 == 0:
                K.op(act, lambda: nc.scalar.copy(out=out, in_=in_), reads, writes)
            else:
                K.op(dve, lambda: nc.vector.tensor_copy(out=out, in_=in_), reads, writes)

        def gemm_fm(wname, nk, c0, ncols, rhs_fn, N, evac, cb=512):
            cb = min(cb, ncols)
            for cblk in range(c0, c0 + ncols, cb):
                wv, wb = wpiece(wname, 0, nk, cblk, cb)
                for mi in range(cb // 128):
                    pap, pb = gbank()
                    for k in range(nk):
                        rap, rbufs = rhs_fn(k)
                        K.mm(pap[:, :N], lhsT=wv[:, k, mi * 128:(mi + 1) * 128], rhs=rap,
                             start=(k == 0), stop=(k == nk - 1), reads=wb + rbufs, pbuf=pb)
                    evac((cblk - c0) // 128 + mi, pap[:, :N], pb)
                wrelease()

        def gemm_tm(wname, nk, c0, ncols, lhs_fn, nsub, evac):
            for cblk in range(c0, c0 + ncols, 512):
                wv, wb = wpiece(wname, 0, nk, cblk, 512)
                for s in range(nsub):
                    pap, pb = gbank()
                    for k in range(nk):
                        lap, lbufs = lhs_fn(k, s)
                        K.mm(pap, lhsT=lap, rhs=wv[:, k, :], start=(k == 0), stop=(k == nk - 1),
                             reads=wb + lbufs, pbuf=pb)
                    evac(s, cblk - c0, pap, pb)
                wrelease()

        def rms_rstd(src_fn, nch, N, inv_n):
            pap, pb = bank("st", [7])
            for c in range(nch):
                sap, sbufs = src_fn(c)
                q = c % 2
                K.op(act, lambda: nc.scalar.activation(out=SQ[q][:, :N], in_=sap, func=AF.Square), sbufs, [SQb[q]])
                K.mm(pap[:, :N], lhsT=ONES[:], rhs=SQ[q][:, :N], start=(c == 0), stop=(c == nch - 1),
                     reads=[SQb[q], CONb], pbuf=pb)
            K.op(act, lambda: nc.scalar.activation(out=RSTD[:, :N], in_=pap[:, :N], func=AF.Sqrt, scale=inv_n, bias=EPSC[:, 0:1]),
                 [pb, CONb], [RSTDb])
            K.op(dve, lambda: nc.vector.reciprocal(out=RSTD[:, :N], in_=RSTD[:, :N]), [RSTDb], [RSTDb])

        EPSC = sb("EPSC", [128, 2], F32)
        K.op(dve, lambda: nc.vector.memset(ONES[:], 1.0), [], [CONb])
        K.op(dve, lambda: nc.vector.memset(EPSC[:, 0:1], EPS), [], [CONb])
        K.op(dve, lambda: nc.vector.memset(SCM[:], 1.0), [], [CONb])
        K.op(dve, lambda: nc.vector.memset(SCM[:].rearrange("p (c t) -> p c t", t=64)[:, :, 0:1], 0.0), [], [CONb])
        K.op(dve, lambda: nc.vector.memset(ST[:], 0.0), [], STb)
        K.op(dve, lambda: nc.vector.memset(STbf[:], 0.0), [], [b for hb in STbfb for b in hb])
        smallb = Buf("small")
        K.dma(pool, IDENT[:], c_id, writes=[Buf("ident")])
        for dst, src in ((Rm, c_R), (TRI, c_tri), (HGM, c_hgm), (ATAB, c_atab), (CM, c_mask), (EBT, c_ebt)):
            K.dma(sp, dst[:], src, writes=[smallb])
        with nc.allow_non_contiguous_dma(reason="tiny param layout loads"):
            for gi in range(5):
                K.dma(sp, G[:, gi, :], gains[gi].rearrange("(c p) -> p c", p=128), writes=[smallb])
            K.dma(sp, SUBG[:], subln.rearrange("(c p) -> p c", p=128), writes=[smallb])
            K.dma(sp, HGN[:], hgnorm.rearrange("(c p) -> p c", p=128), writes=[smallb])
            K.dma(sp, LBL[:], hglb.rearrange("r (c p) -> p r c", p=128), writes=[smallb])
        for r_ in range(4):
            K.dma(sp, LQ[:, r_, :], lamv[r_:r_ + 1, :].to_broadcast([128, 128]), writes=[smallb])
        sm = [smallb]
        K.op(dve, lambda: nc.vector.tensor_scalar(out=SUBG[:], in0=SUBG[:], scalar1=1.0 - LAMBDA_INIT, scalar2=None, op0=ALU.mult), sm, sm)
        K.op(dve, lambda: nc.vector.tensor_tensor(out=LB[:], in0=LBL[:, 0, :], in1=LBL[:, 1, :], op=ALU.subtract), sm, sm)
        K.op(act, lambda: nc.scalar.activation(out=LB[:], in_=LB[:], func=AF.Sigmoid), sm, sm)
        K.op(dve, lambda: nc.vector.tensor_scalar(out=OML[:], in0=LB[:], scalar1=-1.0, scalar2=1.0, op0=ALU.mult, op1=ALU.add), sm, sm)
        K.op(dve, lambda: nc.vector.tensor_tensor(out=LT[:, 0, :], in0=LQ[:, 0, :], in1=LQ[:, 1, :], op=ALU.mult), sm, sm)
        K.op(dve, lambda: nc.vector.tensor_tensor(out=LT[:, 1, :], in0=LQ[:, 2, :], in1=LQ[:, 3, :], op=ALU.mult), sm, sm)
        K.op(dve, lambda: nc.vector.reduce_sum(out=LS[:], in_=LT[:], axis=AX.X), sm, sm)
        K.op(act, lambda: nc.scalar.activation(out=LS[:], in_=LS[:], func=AF.Exp), sm, sm)
        K.op(dve, lambda: nc.vector.scalar_tensor_tensor(out=NLAM[:], in0=LS[:, 1:2], scalar=-LAMBDA_INIT, in1=LS[:, 0:1],
                                                          op0=ALU.add, op1=ALU.subtract), sm, sm)
        K.op(dve, lambda: nc.vector.tensor_scalar(out=CB[:], in0=ATAB[:], scalar1=CM[:, 0:1], scalar2=None, op0=ALU.add), sm, sm)
        K.op(dve, lambda: nc.vector.tensor_scalar(out=EBC[:], in0=EBT[:], scalar1=CM[:, 0:1], scalar2=None, op0=ALU.add), sm, sm)
        K.barrier(hard=True)
        CONb.w = None
        CONb.r = {}
        if not dry:
            pass

        def consts_ready():
            return []

        with ExitStack() as ph:
            MT = ph.enter_context(sbt("MT", [128, NCH, 256], F32))
            MN = ph.enter_context(sbt("MN", [128, NCH, 256], BF16))
            MTb = Buf("MT")
            MNb = [Buf(f"MN{c}") for c in range(NCH)]
            K.dma(sp, MT[:], memT.rearrange("(c p) t -> p c t", p=128), writes=[MTb])
            rms_rstd(lambda c: (MT[:, c, :], [MTb]), NCH, 256, 1.0 / D)
            for c in range(NCH):
                K.op(dve, lambda: nc.vector.scalar_tensor_tensor(out=MN[:, c, :], in0=MT[:, c, :], scalar=G[:, 4, c:c + 1],
                                                                  in1=RSTD[:, :256], op0=ALU.mult, op1=ALU.mult),
                     [MTb, RSTDb], [MNb[c]])
            gemm_fm("w_mem_kv", NCH, 0, 1024, lambda k: (MN[:, k, :], [MNb[k]]), 256,
                    lambda m, p, pb: evac_copy(KX[:, m, :], p, [pb], [KXb]))
            gemm_tm("w_mem_kv", NCH, 1024, 1024, lambda k, s: (MN[:, k, s * 128:(s + 1) * 128], [MNb[k]]), 2,
                    lambda s, co, p, pb: evac_copy(VX[:, s, co:co + 512], p, [pb], [VXb]))
            K.barrier()

        def ffn(pre, gi):
            with ExitStack() as ph:
                HID = ph.enter_context(sbt("HID", [128, NHC, T], BF16))
                HIDb = [Buf(f"HID{j}") for j in range(NHC)]
                rms_rstd(lambda c: (H[:, c, :], [Hb[c]]), NCH, T, 1.0 / D)
                for c in range(NCH):
                    K.op(dve, lambda: nc.vector.scalar_tensor_tensor(out=XN[:, c, :], in0=H[:, c, :], scalar=G[:, gi, c:c + 1],
                                                                      in1=RSTD[:], op0=ALU.mult, op1=ALU.mult),
                         [Hb[c], RSTDb], [XNb[c]])
                for jb in range(NHC // 2):
                    wg, wgb, wu, wub = wpair(pre + "_wg", NCH, jb * 256, pre + "_wu", NCH, jb * 256, 256)
                    for mi in range(2):
                        j = jb * 2 + mi
                        pg, pgb = gbank()
                        pu, pub = gbank()
                        for k in range(NCH):
                            K.mm(pg, lhsT=wg[:, k, mi * 128:(mi + 1) * 128], rhs=XN[:, k, :], start=(k == 0),
                                 stop=(k == NCH - 1), reads=wgb + [XNb[k]], pbuf=pgb)
                        for k in range(NCH):
                            K.mm(pu, lhsT=wu[:, k, mi * 128:(mi + 1) * 128], rhs=XN[:, k, :], start=(k == 0),
                                 stop=(k == NCH - 1), reads=wub + [XNb[k]], pbuf=pub)
                        q = j % 2
                        K.op(act, lambda: nc.scalar.activation(out=SG[q][:], in_=pg, func=AF.Silu), [pgb], [SGb[q]])
                        K.op(dve, lambda: nc.vector.tensor_tensor(out=HID[:, j, :], in0=SG[q][:], in1=pu, op=ALU.mult),
                             [SGb[q], pub], [HIDb[j]])
                    wrelease()
                for mb2 in range(NCH // 2):
                    pbs = [gbank(), gbank()]
                    for half in range(2):
                        wd, wdb = wpiece(pre + "_wd", half * 22, 22, mb2 * 256, 256)
                        for mi in range(2):
                            p, pb = pbs[mi]
                            for jj in range(22):
                                j = half * 22 + jj
                                K.mm(p, lhsT=wd[:, jj, mi * 128:(mi + 1) * 128], rhs=HID[:, j, :], start=(j == 0), stop=(j == NHC - 1),
                                     reads=wdb + [HIDb[j]], pbuf=pb, tick=(jj == 21))
                        wrelease()
                    for mi in range(2):
                        m = mb2 * 2 + mi
                        p, pb = pbs[mi]
                        K.op(dve, lambda: nc.vector.scalar_tensor_tensor(out=H[:, m, :], in0=p, scalar=0.5, in1=H[:, m, :],
                                                                          op0=ALU.mult, op1=ALU.add), [pb, Hb[m]], [Hb[m]])
                K.barrier()

        def load_x(src, t0):
            K.dma(sp, H[:], src.rearrange("(c p) t -> p c t", p=128)[:, :, t0:t0 + T], writes=Hb)

        def mix_norm():
            rms_rstd(lambda c: (H[:, c, :], [Hb[c]]), NCH, T, 1.0 / D)
            for c in range(NCH):
                K.op(dve, lambda: nc.vector.scalar_tensor_tensor(out=XN[:, c, :], in0=H[:, c, :], scalar=G[:, 1, c:c + 1],
                                                                  in1=RSTD[:], op0=ALU.mult, op1=ALU.mult),
                     [Hb[c], RSTDb], [XNb[c]])

        xn_rhs = lambda k: (XN[:, k, :], [XNb[k]])
        xn_lhs = lambda k, s: (XN[:, k, s * 128:(s + 1) * 128], [XNb[k]])

        def da_kv(jt):
            with ExitStack() as ph:
                KTs = ph.enter_context(sbt("KTs", [128, 8, T], BF16))
                VTs = ph.enter_context(sbt("VTs", [128, 4, 4, 256], BF16))
                KTb = [Buf(f"KTs{m}") for m in range(8)]
                VTb = [Buf(f"VTs{s}") for s in range(8)]
                gemm_fm("w_in", NCH, 1024, 1024, xn_rhs, T,
                        lambda m, p, pb: evac_copy(KTs[:, m, :], p, [pb], [KTb[m]]))
                gemm_tm("w_in", NCH, 2048, 1024, xn_lhs, 4,
                        lambda s, co, p, pb: evac_copy(VTs[:, co // 256:co // 256 + 2, s, :], p.rearrange("p (h c) -> p h c", c=256), [pb], [VTb[s * 2 + co // 512]]))
                K.dma(sp, KS[jt], KTs[:], reads=KTb, writes=[KSb[jt]])
                K.dma(sp, VS[jt], VTs[:], reads=VTb, writes=[VSb[jt]])
                K.barrier()

        def hg(own, Y):
            for Gp in range(2):
                with ExitStack() as ph:
                    def t_(name, shape, dt, st=ph):
                        return st.enter_context(sbt(name, shape, dt))
                    IH = t_("hgIH", [128, 4, 512], BF16)
                    KH = t_("hgKH", [128, 4, T], BF16)
                    Et = t_("hgEt", [128, 4, 8], F32)
                    IHb, KHb, Etb = [Buf(f"IH{s}") for s in range(4)], Buf("KH"), Buf("Et")
                    if own:
                        KTb_ = t_("hgKTb", [128, 4, T], BF16)
                        QTb_ = t_("hgQTb", [128, 4, T], BF16)
                        KTbb, QTbb = Buf("KTb"), Buf("QTb")
                    with ExitStack() as ph1:
                        A1 = t_("hgA1", [128, 4, T], F32, ph1)
                        A2 = t_("hgA2", [128, 4, T], F32, ph1)
                        A3 = t_("hgA3", [128, 4, T], F32, ph1)
                        A1b = [Buf(f"A1_{h}") for h in range(4)]
                        A2b, A3b = Buf("A2"), Buf("A3")
                        if own:
                            QH = t_("hgQH", [128, 4, T], F32, ph1)
                            QHb = [Buf(f"QH{h}") for h in range(4)]
                            gemm_fm("w_in", NCH, 3072 + Gp * 512, 512, xn_rhs, T,
                                    lambda m, p, pb: K.op(act, lambda: nc.scalar.activation(out=QH[:, m, :], in_=p, func=AF.Silu), [pb], [QHb[m]]))
                        gemm_fm("w_in", NCH, 4096 + Gp * 512, 512, xn_rhs, T,
                                lambda m, p, pb: K.op(act, lambda: nc.scalar.activation(out=A1[:, m, :], in_=p, func=AF.Sigmoid), [pb], [A1b[m]]))
                        gemm_tm("w_in", NCH, 5120 + Gp * 512, 512, xn_lhs, 4,
                                lambda s, co, p, pb: evac_copy(IH[:, s, :], p, [pb], [IHb[s]]))
                        for hh in range(4):
                            h = Gp * 4 + hh
                            K.op(dve, lambda: nc.vector.tensor_scalar(out=A1[:, hh, :], in0=A1[:, hh, :], scalar1=OML[:, h:h + 1],
                                                                      scalar2=LB[:, h:h + 1], op0=ALU.mult, op1=ALU.add), [A1b[hh]], [A1b[hh]])
                        K.op(act, lambda: nc.scalar.activation(out=A2[:], in_=A1[:], func=AF.Ln), A1b, [A2b])
                        K.op(dve, lambda: nc.vector.tensor_scalar(out=A1[:], in0=A1[:], scalar1=-1.0, scalar2=1.0, op0=ALU.mult, op1=ALU.add), A1b, A1b)
                        for hh in range(4):
                            K.op(dve, lambda: nc.vector.tensor_tensor_scan(out=A3[:, hh, :], data0=SCM[:], data1=A2[:, hh, :], initial=0.0,
                                                                           op0=ALU.mult, op1=ALU.add), [A2b], [A3b])
                        A3v = A3[:].rearrange("p h (c t) -> p h c t", t=64)
                        K.op(act, lambda: nc.scalar.activation(out=Et[:], in_=A3v[:, :, :, 63], func=AF.Exp), [A3b], [Etb])
                        K.op(act, lambda: nc.scalar.activation(out=A2[:], in_=A3[:], func=AF.Exp, scale=-1.0), [A3b, A2b], [A2b])
                        K.op(dve, lambda: nc.vector.tensor_tensor(out=A2[:], in0=A2[:], in1=A1[:], op=ALU.mult), [A2b] + A1b, [A2b])
                        K.op(dve, lambda: nc.vector.tensor_tensor(out=KH[:].rearrange("p h (c t) -> p (h c) t", t=64),
                                                                  in0=A2[:].rearrange("p h (c t) -> p (h c) t", t=64),
                                                                  in1=Et[:].rearrange("p h c -> p (h c)").unsqueeze(2).to_broadcast([128, 32, 64]),
                                                                  op=ALU.mult), [A2b, Etb], [KHb])
                        if own:
                            K.op(act, lambda: nc.scalar.copy(out=KTb_[:], in_=A2[:]), [A2b], [KTbb])
                            K.op(act, lambda: nc.scalar.activation(out=A3[:], in_=A3[:], func=AF.Exp), [A3b], [A3b])
                            K.op(dve, lambda: nc.vector.tensor_tensor(out=QTb_[:], in0=QH[:], in1=A3[:], op=ALU.mult), QHb + [A3b], [QTbb])
                        K.barrier()
                    with ExitStack() as ph2:
                        KHt = t_("hgKHt", [128, 4, 4, 128], BF16, ph2)
                        KHtb = [Buf(f"KHt{h}") for h in range(4)]
                        if own:
                            GH = t_("hgGH", [128, 4, T], BF16, ph2)
                            AT = [t_(f"hgAT{i_}", [128, 128], BF16, ph2) for i_ in range(4)]
                            OH = t_("hgOH", [128, 4, T], F32, ph2)
                            GHb = [Buf(f"GH{h}") for h in range(4)]
                            ATb = [Buf(f"AT{i_}") for i_ in range(4)]
                            OHb = [Buf(f"OH{h}") for h in range(4)]
                            gemm_fm("w_in", NCH, 6144 + Gp * 512, 512, xn_rhs, T,
                                    lambda m, p, pb: K.op(act, lambda: nc.scalar.activation(out=GH[:, m, :], in_=p, func=AF.Sigmoid), [pb], [GHb[m]]))
                        for hh in range(4):
                            pap, pb = gbank()
                            for s in range(4):
                                K.mm(pap[:, s * 128:(s + 1) * 128], lhsT=KH[:, hh, s * 128:(s + 1) * 128], rhs=IDENT[:],
                                     start=True, stop=True, reads=[KHb], pbuf=pb)
                            evac_copy(KHt[:, hh, :, :], pap.rearrange("p (s k) -> p s k", k=128), [pb], [KHtb[hh]])
                        OB = [0, 1, 2, 3]
                        for s in range(4):
                            if own:
                                for hh in range(4):
                                    sp_, spb = bank("hs", [4, 5])
                                    K.mm(sp_[:, 0:128], lhsT=KTb_[:, hh, s * 128:(s + 1) * 128], rhs=QTb_[:, hh, s * 128:(s + 1) * 128],
                                         start=True, stop=True, reads=[KTbb, QTbb], pbuf=spb)
                                    K.op(dve, lambda: nc.vector.tensor_tensor(out=AT[hh][:], in0=sp_[:, 0:128], in1=HGM[:], op=ALU.mult), [spb], [ATb[hh]])
                                for hh in range(4):
                                    K.mm(PS[:, OB[hh], 0:128], lhsT=IH[:, s, hh * 128:(hh + 1) * 128], rhs=AT[hh][:], start=True, stop=False,
                                         reads=[IHb[s], ATb[hh]], pbuf=PB[OB[hh]])
                            for c2 in range(2):
                                ch = s * 2 + c2
                                par = ch % 2
                                if own:
                                    for hh in range(4):
                                        h = Gp * 4 + hh
                                        K.mm(PS[:, OB[hh], c2 * 64:(c2 + 1) * 64], lhsT=STbf[:, h, par, :],
                                             rhs=QTb_[:, hh, s * 128 + c2 * 64:s * 128 + (c2 + 1) * 64], start=False, stop=(c2 == 1),
                                             reads=[STbfb[h][par], QTbb], pbuf=PB[OB[hh]], tick=True)
                                for hh in range(4):
                                    h = Gp * 4 + hh
                                    pp, ppb = bank("hp", [6, 7] if own else [4, 5, 6, 7])
                                    K.mm(pp[:, 0:128], lhsT=KHt[c2 * 64:(c2 + 1) * 64, hh, s, :], rhs=IH[c2 * 64:(c2 + 1) * 64, s, hh * 128:(hh + 1) * 128],
                                         start=True, stop=True, reads=[KHtb[hh], IHb[s]], pbuf=ppb)
                                    K.op(dve, lambda: nc.vector.scalar_tensor_tensor(out=ST[:, h, :], in0=ST[:, h, :], scalar=Et[:, hh, ch:ch + 1],
                                                                                      in1=pp[:, 0:128], op0=ALU.mult, op1=ALU.add),
                                         [STb[h], Etb, ppb], [STb[h]])
                                    K.op(act, lambda: nc.scalar.copy(out=STbf[:, h, 1 - par, :], in_=ST[:, h, :]), [STb[h]], [STbfb[h][1 - par]])
                            if own:
                                for hh in range(4):
                                    evac_copy(OH[:, hh, s * 128:(s + 1) * 128], PS[:, OB[hh], 0:128], [PB[OB[hh]]], [OHb[hh]])
                        if own:
                            SQ4 = KTb_
                            K.op(act, lambda: nc.scalar.activation(out=SQ4[:], in_=OH[:], func=AF.Square), OHb + [KTbb], [KTbb])
                            for hh in range(4):
                                h = Gp * 4 + hh
                                pap, pb = gbank()
                                K.mm(pap, lhsT=ONES[:], rhs=SQ4[:, hh, :], start=True, stop=True, reads=[KTbb], pbuf=pb)
                                q = hh % 2
                                K.op(act, lambda: nc.scalar.activation(out=SG[q][:], in_=pap, func=AF.Sqrt, scale=1.0 / 128, bias=EPSC[:, 0:1]), [pb], [SGb[q]])
                                K.op(dve, lambda: nc.vector.reciprocal(out=SG[q][:], in_=SG[q][:]), [SGb[q]], [SGb[q]])
                                K.op(dve, lambda: nc.vector.tensor_tensor(out=OH[:, hh, :], in0=OH[:, hh, :], in1=SG[q][:], op=ALU.mult), [OHb[hh], SGb[q]], [OHb[hh]])
                                K.op(dve, lambda: nc.vector.scalar_tensor_tensor(out=Y[0][:, 8 + h, :], in0=OH[:, hh, :], scalar=HGN[:, h:h + 1], in1=GH[:, hh, :],
                                                                                  op0=ALU.mult, op1=ALU.mult), [OHb[hh], GHb[hh]], [Y[1][8 + h]])
                        K.barrier()

        def da_attn(i, Y):
            nkt = NT + i + 1
            with ExitStack() as ph:
                def t_(name, shape, dt):
                    return ph.enter_context(sbt(name, shape, dt))
                QT = t_("daQT", [128, 8, T], BF16)
                QTb = [Buf(f"daQT{m}") for m in range(8)]
                gemm_fm("w_in", NCH, 0, 1024, xn_rhs, T,
                        lambda m, p, pb: evac_copy(QT[:, m, :], p, [pb], [QTb[m]], scale=128 ** -0.5))
                VBh = t_("daVB", [128, 2 * NT, 4, 256], BF16)
                KB = t_("daKB0", [128, 2 * NT, T], BF16)
                TT = [t_(f"daTT{i_}", [128, T], F32) for i_ in range(2)] + [SG[0], SG[1]]
                PT = [t_(f"daPT{i_}", [128, T], BF16) for i_ in range(4)]
                RD = RSTD
                O0 = t_("daO0", [128, 2, T], F32)
                YD = t_("daYD", [128, 2, T], F32)
                VBb, KBb = Buf("VBh"), Buf("KB0")
                TTb = [Buf("TT0"), Buf("TT1"), SGb[0], SGb[1]]
                PTb = [Buf(f"PT{i_}") for i_ in range(4)]
                RDb, O0b, YDb = RSTDb, [Buf("O0_0"), Buf("O0_1")], [Buf("YD0"), Buf("YD1")]
                all_blocks = [(j, kb) for j in range(nkt) for kb in range(4)]

                def bdist(j, kb):
                    return (j - (NT + i)) * 512 + kb * 128

                MIN_DIST = {0: -512, 1: -1792}
                DEPTH = 3
                SB = [0, 1, 2, 3, 4]
                accs = [5, 6, 7]
                for h in range(4):
                    K.dma(sp, VBh[:, 0:nkt, :, :].rearrange("p j s c -> p j (s c)"), VS[0:nkt, :, h, :, :].rearrange("j p s c -> p j (s c)"),
                          reads=VSb[:nkt], writes=[VBb])
                    blocks = [(j, kb) for (j, kb) in all_blocks if bdist(j, kb) >= MIN_DIST.get(h, -10 ** 9)]
                    nb = len(blocks)
                    for c in range(2):
                        hc = h * 2 + c
                        K.dma(sp, KB[:, 0:nkt, :], KS[0:nkt, :, hc, :].rearrange("j p t -> p j t"), reads=KSb[:nkt], writes=[KBb])

                        def geom(bi):
                            j, kb = blocks[bi]
                            diag = (j == nkt - 1)
                            qs = kb * 128 if diag else 0
                            return j, kb, diag, qs, T - qs

                        def s_stage(bi):
                            j, kb, diag, qs, n = geom(bi)
                            sp_, spb = bank("das", SB)
                            K.mm(sp_[:, :n], lhsT=KB[:, j, kb * 128:(kb + 1) * 128], rhs=QT[:, hc, qs:], start=True, stop=True,
                                 reads=[KBb, QTb[hc]], pbuf=spb)
                            return sp_, spb

                        sq_ = {}
                        for bi in range(min(DEPTH, nb)):
                            sq_[bi] = s_stage(bi)
                        for bi in range(nb):
                            if bi + DEPTH < nb:
                                sq_[bi + DEPTH] = s_stage(bi + DEPTH)
                            sp_, spb = sq_.pop(bi)
                            j, kb, diag, qs, n = geom(bi)
                            dist = bdist(j, kb)
                            q = bi % 4
                            if h == 0:
                                K.op(dve, lambda: nc.vector.scalar_tensor_tensor(out=TT[q][:, :n], in0=Rm[:, :n], scalar=SLOPES[h], in1=sp_[:, :n],
                                                                                  op0=ALU.mult, op1=ALU.add), [spb], [TTb[q]])
                                if diag:
                                    K.op(dve, lambda: nc.vector.tensor_tensor(out=TT[q][:, 0:128], in0=TT[q][:, 0:128], in1=TRI[:], op=ALU.add), [TTb[q]], [TTb[q]])
                                    bias = 0.0
                                elif j < NT:
                                    nidx = (-dist) // 128
                                    bias = CB[:, h * 32 + nidx:h * 32 + nidx + 1]
                                else:
                                    bias = float(SLOPES[h] * dist)
                                K.op(act, lambda: nc.scalar.activation(out=PT[q][:, :n], in_=TT[q][:, :n], func=AF.Exp, bias=bias), [TTb[q]], [PTb[q]])
                            else:
                                col = h * 36 + 32 + dist // 128
                                bcol = (EBC if j < NT else EBT)[:, col:col + 1]
                                if diag:
                                    K.op(dve, lambda: nc.vector.tensor_tensor(out=TT[q][:, 0:128], in0=sp_[:, 0:128], in1=TRI[:], op=ALU.add), [spb], [TTb[q]])
                                    K.op(act, lambda: nc.scalar.activation(out=PT[q][:, 0:128], in_=TT[q][:, 0:128], func=AF.Exp, bias=bcol), [TTb[q]], [PTb[q]])
                                    if n > 128:
                                        K.op(act, lambda: nc.scalar.activation(out=PT[q][:, 128:n], in_=sp_[:, 128:n], func=AF.Exp, bias=bcol), [spb], [PTb[q]])
                                else:
                                    K.op(act, lambda: nc.scalar.activation(out=PT[q][:, :n], in_=sp_[:, :n], func=AF.Exp, bias=bcol), [spb], [PTb[q]])
                            last = (bi == nb - 1)
                            for e in range(2):
                                K.mm(PS[:, accs[e], qs:], lhsT=VBh[:, j, kb, e * 128:(e + 1) * 128], rhs=PT[q][:, :n], start=(bi == 0), stop=last,
                                     reads=[VBb, PTb[q]], pbuf=PB[accs[e]])
                            K.mm(PS[:, accs[2], qs:], lhsT=ONES[:], rhs=PT[q][:, :n], start=(bi == 0), stop=last, reads=[PTb[q]], pbuf=PB[accs[2]])
                        K.op(dve, lambda: nc.vector.reciprocal(out=RD[:], in_=PS[:, accs[2], :]), [PB[accs[2]]], [RDb])
                        for e in range(2):
                            if c == 0:
                                K.op(dve, lambda: nc.vector.tensor_tensor(out=O0[:, e, :], in0=PS[:, accs[e], :], in1=RD[:], op=ALU.mult),
                                     [PB[accs[e]], RDb], [O0b[e]])
                            else:
                                K.op(dve, lambda: nc.vector.tensor_tensor(out=YD[:, e, :], in0=PS[:, accs[e], :], in1=RD[:], op=ALU.mult),
                                     [PB[accs[e]], RDb], [YDb[e]])
                                K.op(dve, lambda: nc.vector.scalar_tensor_tensor(out=YD[:, e, :], in0=YD[:, e, :], scalar=NLAM[:, 0:1], in1=O0[:, e, :],
                                                                                  op0=ALU.mult, op1=ALU.add), [YDb[e], O0b[e]], [YDb[e]])
                        if c == 1:
                            pap, pb = bank("das", SB)
                            for e in range(2):
                                K.op(act, lambda: nc.scalar.activation(out=SQ[e][:], in_=YD[:, e, :], func=AF.Square), [YDb[e]], [SQb[e]])
                                K.mm(pap, lhsT=ONES[:], rhs=SQ[e][:], start=(e == 0), stop=(e == 1), reads=[SQb[e]], pbuf=pb)
                            K.op(act, lambda: nc.scalar.activation(out=RD[:], in_=pap, func=AF.Sqrt, scale=1.0 / 256, bias=EPSC[:, 0:1]), [pb], [RDb])
                            K.op(dve, lambda: nc.vector.reciprocal(out=RD[:], in_=RD[:]), [RDb], [RDb])
                            for e in range(2):
                                K.op(dve, lambda: nc.vector.scalar_tensor_tensor(out=Y[0][:, h * 2 + e, :], in0=YD[:, e, :], scalar=SUBG[:, e:e + 1], in1=RD[:],
                                                                                  op0=ALU.mult, op1=ALU.mult), [YDb[e], RDb], [Y[1][h * 2 + e]])
                K.barrier()

        def xa(Y):
            with ExitStack() as ph:
                QX = ph.enter_context(sbt("xaQX", [128, 8, T], BF16))
                PT = [ph.enter_context(sbt(f"xaPT{i_}", [128, T], BF16)) for i_ in range(2)]
                RD = ph.enter_context(sbt("xaRD", [128, T], F32))
                QXb = [Buf(f"QX{m}") for m in range(8)]
                PTb = [Buf("xPT0"), Buf("xPT1")]
                RDb = Buf("xRD")
                gemm_fm("w_in", NCH, 7168, 1024, xn_rhs, T,
                        lambda m, p, pb: evac_copy(QX[:, m, :], p, [pb], [QXb[m]], scale=256 ** -0.5))
                xblocks = [(h, mb) for h in range(4) for mb in range(2)]
                SB = [0, 1, 2, 3, 4]
                accs = [5, 6, 7]

                def xs_stage(bi):
                    h, mb = xblocks[bi]
                    sp_, spb = bank("das", SB)
                    for dc in range(2):
                        K.mm(sp_, lhsT=KX[:, h * 2 + dc, mb * 128:(mb + 1) * 128], rhs=QX[:, h * 2 + dc, :], start=(dc == 0), stop=(dc == 1),
                             reads=[KXb, QXb[h * 2 + dc]], pbuf=spb)
                    return sp_, spb

                xq = {0: xs_stage(0), 1: xs_stage(1)}
                for bi, (h, mb) in enumerate(xblocks):
                    if bi + 2 < len(xblocks):
                        xq[bi + 2] = xs_stage(bi + 2)
                    sp_, spb = xq.pop(bi)
                    K.op(act, lambda: nc.scalar.activation(out=PT[mb][:], in_=sp_, func=AF.Exp), [spb], [PTb[mb]])
                    for e in range(2):
                        K.mm(PS[:, accs[e], :], lhsT=VX[:, mb, h * 256 + e * 128:h * 256 + (e + 1) * 128], rhs=PT[mb][:], start=(mb == 0), stop=(mb == 1),
                             reads=[VXb, PTb[mb]], pbuf=PB[accs[e]])
                    K.mm(PS[:, accs[2], :], lhsT=ONES[:], rhs=PT[mb][:], start=(mb == 0), stop=(mb == 1), reads=[PTb[mb]], pbuf=PB[accs[2]])
                    if mb == 1:
                        K.op(dve, lambda: nc.vector.reciprocal(out=RD[:], in_=PS[:, accs[2], :]), [PB[accs[2]]], [RDb])
                        for e in range(2):
                            K.op(dve, lambda: nc.vector.tensor_tensor(out=Y[0][:, 16 + h * 2 + e, :], in0=PS[:, accs[e], :], in1=RD[:], op=ALU.mult),
                                 [PB[accs[e]], RDb], [Y[1][16 + h * 2 + e]])
                K.barrier()

        def merge_out(Y):
            with ExitStack() as ph:
                MACC = ph.enter_context(sbt("MACC", [128, 4, T], F32))
                MRG = ph.enter_context(sbt("MRG", [128, NCH, T], BF16))
                MACCb = [Buf(f"MACC{m}") for m in range(4)]
                MRGb = [Buf(f"MRG{m}") for m in range(NCH)]
                for mb2 in range(8):
                    for b in range(3):
                        gw, gwb, bw, bwb = wpair("w_in", NCH, 8192 + b * 2048 + mb2 * 256, f"wb{b}", 8, mb2 * 256, 256)
                        for mi in range(2):
                            m = mb2 * 2 + mi
                            pg, pgb = gbank()
                            pr, prb = gbank()
                            for k in range(NCH):
                                K.mm(pg, lhsT=gw[:, k, mi * 128:(mi + 1) * 128], rhs=XN[:, k, :], start=(k == 0), stop=(k == NCH - 1),
                                     reads=gwb + [XNb[k]], pbuf=pgb)
                            for k in range(8):
                                K.mm(pr, lhsT=bw[:, k, mi * 128:(mi + 1) * 128], rhs=Y[0][:, b * 8 + k, :], start=(k == 0), stop=(k == 7),
                                     reads=bwb + [Y[1][b * 8 + k]], pbuf=prb)
                            q = (b * 2 + mi) % 2
                            K.op(act, lambda: nc.scalar.activation(out=SG[q][:], in_=pg, func=AF.Sigmoid), [pgb], [SGb[q]])
                            if b == 0:
                                K.op(dve, lambda: nc.vector.tensor_tensor(out=MACC[:, mi, :], in0=SG[q][:], in1=pr, op=ALU.mult), [SGb[q], prb], [MACCb[mi]])
                            else:
                                K.op(dve, lambda: nc.vector.tensor_tensor(out=SG[q][:], in0=SG[q][:], in1=pr, op=ALU.mult), [SGb[q], prb], [SGb[q]])
                                if b == 1:
                                    K.op(dve, lambda: nc.vector.tensor_tensor(out=MACC[:, mi, :], in0=MACC[:, mi, :], in1=SG[q][:], op=ALU.add),
                                         [SGb[q], MACCb[mi]], [MACCb[mi]])
                                else:
                                    K.op(dve, lambda: nc.vector.tensor_tensor(out=MRG[:, m, :], in0=MACC[:, mi, :], in1=SG[q][:], op=ALU.add),
                                         [SGb[q], MACCb[mi]], [MRGb[m]])
                        wrelease()
                gemm_fm("w_out", NCH, 0, D, lambda k: (MRG[:, k, :], [MRGb[k]]), T,
                        lambda m, p, pb: K.op(dve, lambda: nc.vector.tensor_tensor(out=H[:, m, :], in0=p, in1=H[:, m, :], op=ALU.add), [pb, Hb[m]], [Hb[m]]))
                K.barrier()

        OUTC = [sb(f"OUTC{i_}", [128, T], F32) for i_ in range(2)]
        OUTCb = [Buf("OUTC0"), Buf("OUTC1")]

        def final_out(t0, gi=3, do_norm=True, next_t0=None):
            if do_norm:
                rms_rstd(lambda c: (H[:, c, :], [Hb[c]]), NCH, T, 1.0 / D)
            ov = outT.rearrange("(c p) t -> p c t", p=128)
            for c in range(NCH):
                q = c % 2
                if do_norm:
                    K.op(dve, lambda: nc.vector.scalar_tensor_tensor(out=OUTC[q][:], in0=H[:, c, :], scalar=G[:, gi, c:c + 1], in1=RSTD[:],
                                                                      op0=ALU.mult, op1=ALU.mult), [Hb[c], RSTDb], [OUTCb[q]])
                else:
                    K.op(dve, lambda: nc.vector.tensor_copy(out=OUTC[q][:], in_=H[:, c, :]), [Hb[c]], [OUTCb[q]])
                K.dma(sp, ov[:, c, t0:t0 + T], OUTC[q][:], reads=[OUTCb[q]])
                if next_t0 is not None:
                    K.dma(sp, H[:, c, :], xo[c * 128:(c + 1) * 128, next_t0:next_t0 + T], writes=[Hb[c]])
            K.barrier()

        stage = debug_stage
        if stage == "ffn1":
            for i in range(NT):
                load_x(xo, i * T)
                ffn("ffn1", 0)
                final_out(i * T, do_norm=False)
        else:
            load_x(xc, 0)
            for j in range(NT):
                ffn("ffn1", 0)
                mix_norm()
                if j + 1 < NT:
                    load_x(xc, (j + 1) * T)
                else:
                    load_x(xo, 0)
                da_kv(j)
                hg(False, None)
            for i in range(NT):
                ffn("ffn1", 0)
                mix_norm()
                da_kv(NT + i)
                with ExitStack() as yph:
                    Yt = yph.enter_context(sbt("Y", [128, 24, T], BF16))
                    Y = (Yt, [Buf(f"Y{m}") for m in range(24)])
                    da_attn(i, Y)
                    hg(True, Y)
                    xa(Y)
                    merge_out(Y)
                ffn("ffn2", 2)
                final_out(i * T, next_t0=(i + 1) * T if i + 1 < NT else None)
        if not dry:
            for ev in list(K.sp_dma.values()):
                sp.wait(ev)
            for b in OUTCb:
                for ev in b.r.values():
                    sp.wait(ev)
        build.stats = (K.n_inst, K.nsem)
    return nc


_CONST_CACHE = {}


def _consts():
    if _CONST_CACHE:
        return _CONST_CACHE
    ki = np.arange(128, dtype=np.float32)[:, None]
    qi = np.arange(512, dtype=np.float32)[None, :]
    R = (ki - qi).astype(np.float32)
    tri = np.where(ki <= qi[:, :128], 0.0, NEG).astype(np.float32)
    s_ = np.arange(128)[:, None]
    t_ = np.arange(128)[None, :]
    hgm = ((s_ // 64 == t_ // 64) & (s_ <= t_)).astype(np.float32)
    ident = np.eye(128, dtype=np.float32)
    atab = np.zeros((128, 128), np.float32)
    for h in range(4):
        for n in range(32):
            atab[:, h * 32 + n] = -SLOPES[h] * 128.0 * n
    ebt = np.zeros((128, 144), np.float32)
    for h in range(4):
        for d in range(36):
            ebt[:, h * 36 + d] = SLOPES[h] * (np.arange(128, dtype=np.float32) + (d - 32) * 128.0)
    _CONST_CACHE.update(c_R=R, c_tri=tri, c_hgm=hgm, c_id=ident, c_atab=atab, c_ebt=ebt)
    return _CONST_CACHE


def _in_maps(inputs):
    x = np.asarray(inputs["x"], np.float32)
    mem = np.asarray(inputs["mem"], np.float32)
    w = {
        "ffn1_wg": inputs["ffn1_w_gate"][0], "ffn1_wu": inputs["ffn1_w_up"][0], "ffn1_wd": inputs["ffn1_w_down"][0],
        "w_in": inputs["w_in"][0], "w_mem_kv": inputs["w_mem_kv"][0], "wb0": inputs["w_branch_da"][0],
        "wb1": inputs["w_branch_hg"][0], "wb2": inputs["w_branch_xa"][0], "w_out": inputs["w_out"][0],
        "ffn2_wg": inputs["ffn2_w_gate"][0], "ffn2_wu": inputs["ffn2_w_up"][0], "ffn2_wd": inputs["ffn2_w_down"][0],
    }
    w = {k: np.ascontiguousarray(np.asarray(v, np.float32)) for k, v in w.items()}
    gains = np.ascontiguousarray(np.stack([np.asarray(inputs["ffn1_norm"][0]), np.asarray(inputs["mix_norm"][0]),
                                           np.asarray(inputs["ffn2_norm"][0]), np.asarray(inputs["final_norm"]),
                                           np.asarray(inputs["mem_norm"][0])]).astype(np.float32))
    lamv = np.ascontiguousarray(np.stack([np.asarray(inputs["da_lambda_q1"][0]), np.asarray(inputs["da_lambda_k1"][0]),
                                          np.asarray(inputs["da_lambda_q2"][0]), np.asarray(inputs["da_lambda_k2"][0])]).astype(np.float32))
    common = dict(w)
    common.update(gains=gains, subln=np.ascontiguousarray(np.asarray(inputs["da_subln"][0], np.float32)),
                  hgnorm=np.ascontiguousarray(np.asarray(inputs["hg_norm"][0], np.float32)),
                  hglb=np.ascontiguousarray(np.asarray(inputs["hg_lb_logits"], np.float32)), lamv=lamv)
    common.update(_consts())
    maps = []
    for c in range(8):
        b, half = c // 2, c % 2
        m = dict(common)
        m["xo"] = np.ascontiguousarray(x[b, half * OWN:(half + 1) * OWN, :].T)
        if half == 1:
            m["xc"] = np.ascontiguousarray(x[b, 0:OWN, :].T)
            m["c_mask"] = np.zeros((128, 1), np.float32)
        else:
            m["xc"] = np.zeros((D, OWN), np.float32)
            m["c_mask"] = np.full((128, 1), NEG, np.float32)
        m["memT"] = np.ascontiguousarray(mem[b].T)
        maps.append(m)
    return maps


_NC_CACHE = {}


def _get_nc(stage=None):
    if stage not in _NC_CACHE:
        plan = []
        build(True, plan, stage)
        _NC_CACHE[stage] = build(False, plan, stage)
    return _NC_CACHE[stage]


def kernel(**inputs):
    nc = _get_nc(None)
    maps = _in_maps(inputs)
    res = run_bass_kernel_spmd(nc, maps, core_ids=list(range(8)))
    out = np.empty((4, 4096, D), np.float32)
    for c in range(8):
        b, half = c // 2, c % 2
        out[b, half * OWN:(half + 1) * OWN, :] = np.asarray(res.results[c]["outT"]).T
    return out
```
